# Optimizing a Trainium2 kernel written in Bass

```python
import jax, jax.numpy as jnp
from jax import lax
import numpy as np

D_MODEL = 1024
BATCH = 16
SEQ = 4096
DEPTH = 2

D_CONV = D_MODEL
CONV_WIDTH = 31
D_SGU = D_MODEL
SGU_GROUPS = 8
SGU_GROUP_DIM = D_SGU // SGU_GROUPS
CHUNK = 128
D_FF = ((8 * D_MODEL // 3 + 255) // 256) * 256
D_IN = 2 * D_CONV + 2 * D_SGU + 2 * D_MODEL
EPS = 1e-6

kernel_name = "hybrid_conformer_conv_gmlp_encoder"


def rmsnorm(x, g):
    xf = x.astype(jnp.float32)
    y = xf * lax.rsqrt(jnp.mean(xf * xf, axis=-1, keepdims=True) + EPS)
    return (y * g.astype(jnp.float32)).astype(x.dtype)


def layernorm(x, g, b):
    xf = x.astype(jnp.float32)
    mu = jnp.mean(xf, axis=-1, keepdims=True)
    var = jnp.mean(jnp.square(xf - mu), axis=-1, keepdims=True)
    y = (xf - mu) * lax.rsqrt(var + EPS)
    return (y * g.astype(jnp.float32) + b.astype(jnp.float32)).astype(x.dtype)


def depthwise_conv(x, w, b):
    pad = (CONV_WIDTH - 1) // 2
    y = lax.conv_general_dilated(
        x, w.astype(x.dtype), window_strides=(1,), padding=[(pad, pad)],
        dimension_numbers=("NWC", "WIO", "NWC"), feature_group_count=x.shape[-1])
    return y + b


def conformer_conv_branch(val, gate, conv_w, conv_b, ln_g, ln_b, w_out):
    c = val * jax.nn.sigmoid(gate)
    c = depthwise_conv(c, conv_w, conv_b)
    c = jax.nn.silu(layernorm(c, ln_g, ln_b))
    return c @ w_out


def spatial_gating_branch(u, v, ln_g, ln_b, w_s, b_s, w_out):
    bsz, seq, _ = v.shape
    n_chunks = seq // CHUNK
    vn = layernorm(v, ln_g, ln_b)
    vc = vn.reshape(bsz, n_chunks, CHUNK, SGU_GROUPS, SGU_GROUP_DIM)
    mixed = jnp.einsum("bcpgd,gqp->bcqgd", vc, w_s.astype(vc.dtype))
    mixed = mixed + jnp.transpose(b_s)[None, None, :, :, None]
    gated = u * mixed.reshape(bsz, seq, D_SGU)
    return gated @ w_out


def setup_inputs(seed: int = 0) -> dict:
    key = jax.random.key(seed)
    ks = jax.random.split(key, 24)
    f32 = jnp.float32

    def nrm(k, shape, scale):
        return jax.random.normal(k, shape, f32) * scale

    L = DEPTH
    return {
        "x": jax.random.normal(ks[0], (BATCH, SEQ, D_MODEL), f32),
        "norm_mix": 1.0 + nrm(ks[1], (L, D_MODEL), 0.02),
        "w_in": nrm(ks[2], (L, D_MODEL, D_IN), D_MODEL ** -0.5),
        "gate_bias": nrm(ks[3], (L, 2 * D_MODEL), 0.02),
        "conv_w": nrm(ks[4], (L, CONV_WIDTH, 1, D_CONV), CONV_WIDTH ** -0.5),
        "conv_b": nrm(ks[5], (L, D_CONV), 0.02),
        "conv_ln_g": 1.0 + nrm(ks[6], (L, D_CONV), 0.02),
        "conv_ln_b": nrm(ks[7], (L, D_CONV), 0.02),
        "w_conv_out": nrm(ks[8], (L, D_CONV, D_MODEL), D_CONV ** -0.5),
        "sgu_ln_g": 1.0 + nrm(ks[9], (L, D_SGU), 0.02),
        "sgu_ln_b": nrm(ks[10], (L, D_SGU), 0.02),
        "w_spatial": nrm(ks[11], (L, SGU_GROUPS, CHUNK, CHUNK), CHUNK ** -0.5),
        "b_spatial": 1.0 + nrm(ks[12], (L, SGU_GROUPS, CHUNK), 0.02),
        "w_sgu_out": nrm(ks[13], (L, D_SGU, D_MODEL), D_SGU ** -0.5),
        "w_o": nrm(ks[14], (L, D_MODEL, D_MODEL), D_MODEL ** -0.5),
        "norm_ffn": 1.0 + nrm(ks[15], (L, D_MODEL), 0.02),
        "w_ffn_gate": nrm(ks[16], (L, D_MODEL, D_FF), D_MODEL ** -0.5),
        "w_ffn_up": nrm(ks[17], (L, D_MODEL, D_FF), D_MODEL ** -0.5),
        "w_ffn_down": nrm(ks[18], (L, D_FF, D_MODEL), D_FF ** -0.5),
        "norm_final": 1.0 + nrm(ks[19], (D_MODEL,), 0.02),
    }


def reference(x, norm_mix, w_in, gate_bias, conv_w, conv_b, conv_ln_g, conv_ln_b,
              w_conv_out, sgu_ln_g, sgu_ln_b, w_spatial, b_spatial, w_sgu_out, w_o,
              norm_ffn, w_ffn_gate, w_ffn_up, w_ffn_down, norm_final):
    split_points = [D_CONV, 2 * D_CONV, 2 * D_CONV + D_SGU, 2 * D_CONV + 2 * D_SGU,
                    2 * D_CONV + 2 * D_SGU + D_MODEL]
    for l in range(DEPTH):
        h = rmsnorm(x, norm_mix[l])
        proj = h @ w_in[l]
        a_val, a_gate, u, v, g_a, g_b = jnp.split(proj, split_points, axis=-1)
        y_a = conformer_conv_branch(a_val, a_gate, conv_w[l], conv_b[l],
                                    conv_ln_g[l], conv_ln_b[l], w_conv_out[l])
        y_b = spatial_gating_branch(u, v, sgu_ln_g[l], sgu_ln_b[l],
                                    w_spatial[l], b_spatial[l], w_sgu_out[l])
        gb_a, gb_b = jnp.split(gate_bias[l], 2)
        merged = jax.nn.sigmoid(g_a + gb_a) * y_a + jax.nn.sigmoid(g_b + gb_b) * y_b
        x = x + merged @ w_o[l]
        h2 = rmsnorm(x, norm_ffn[l])
        x = x + (jax.nn.silu(h2 @ w_ffn_gate[l]) * (h2 @ w_ffn_up[l])) @ w_ffn_down[l]
    return rmsnorm(x, norm_final)
```

```python
import numpy as np
from contextlib import ExitStack
import concourse.bass as bass
import concourse.mybir as mybir
from concourse.bass_utils import run_bass_kernel_spmd

F32 = mybir.dt.float32
BF16 = mybir.dt.bfloat16
AF = mybir.ActivationFunctionType
ALU = mybir.AluOpType

D = 1024
KC = 8
DFF = 2816
NJ = 22
CW = 31
HALO = 15
T = 512
TW = T + 2 * HALO
EPS = 1e-6
NCORES = 8

OFF_V = 0
OFF_U = 64
OFF_GV = 128
GVLEN = 128 + 40
OFF_GAB = OFF_GV + GVLEN
OFF_CU5 = OFF_GAB + 128
OFF_MIX = OFF_CU5 + 24
OFF_WO = OFF_MIX + 128
OFF_FFN = OFF_WO + 64
OFF_DOWN = OFF_FFN + 352
NB = OFF_DOWN + 176
CHB = 32
RING = 5
CPB = 16
CWIN = 4
CSEM = 8
NPIECE = (NB + CPB - 1) // CPB
NCS = 2
NCST = 4
CSW = 540
NPL = 72


def _blk(W):
    K, M = W.shape
    return W.reshape(K // 128, 128, M // 128, 128).transpose(1, 0, 2, 3)


def _layer_blocks(w_in, conv_w, w_conv_out, w_sgu_out, w_o, w_gate, w_up, w_down):
    out = np.zeros((128, NB, 128), np.float32)
    bv = _blk(w_in[:, 3072:4096])
    out[:, OFF_V:OFF_V + 64] = bv.reshape(128, 8, 2, 4, 128).transpose(0, 2, 1, 3, 4).reshape(128, 64, 128)
    out[:, OFF_U:OFF_U + 64] = _blk(w_in[:, 2048:3072]).transpose(0, 2, 1, 3).reshape(128, 64, 128)
    bg = _blk(w_in[:, 1024:2048]).transpose(0, 2, 1, 3)
    bva = _blk(w_in[:, 0:1024]).transpose(0, 2, 1, 3)
    wpad = np.zeros((32, 1024), np.float32)
    wpad[:CW] = conv_w.reshape(CW, 1024)
    w5 = wpad.reshape(8, 4, 8, 4, 32)
    cu = np.zeros((8, 4, 32, 8, 4, 32), np.float32)
    ii = np.arange(32)
    cu[:, :, ii, :, :, ii] = w5.transpose(4, 2, 1, 0, 3)
    cu = cu.reshape(8, 128, 8, 128).transpose(1, 0, 2, 3)
    cu = cu.reshape(128, 8, 1024).reshape(128, 8, 8, 128)
    pos = OFF_GV
    for m in range(8):
        out[:, pos:pos + 8] = bg[:, m]
        out[:, pos + 8:pos + 16] = bva[:, m]
        pos += 16
        if m >= 3:
            out[:, pos:pos + 8] = cu[:, m - 3]
            pos += 8
    assert pos == OFF_GAB
    for m in (5, 6, 7):
        out[:, OFF_CU5 + (m - 5) * 8:OFF_CU5 + (m - 4) * 8] = cu[:, m]
    ba = _blk(w_in[:, 4096:5120]).transpose(0, 2, 1, 3)
    bb = _blk(w_in[:, 5120:6144]).transpose(0, 2, 1, 3)
    out[:, OFF_GAB:OFF_GAB + 128] = np.stack([ba, bb], axis=2).reshape(128, 128, 128)
    bs = _blk(w_sgu_out).transpose(0, 2, 1, 3)
    bc = _blk(w_conv_out).transpose(0, 2, 1, 3)
    out[:, OFF_MIX:OFF_MIX + 128] = np.stack([bs, bc], axis=2).reshape(128, 128, 128)
    out[:, OFF_WO:OFF_WO + 64] = _blk(w_o).transpose(0, 2, 1, 3).reshape(128, 64, 128)
    fg = _blk(w_gate).transpose(0, 2, 1, 3)
    fu = _blk(w_up).transpose(0, 2, 1, 3)
    out[:, OFF_FFN:OFF_FFN + 352] = np.stack([fg, fu], axis=2).reshape(128, 352, 128)
    out[:, OFF_DOWN:OFF_DOWN + 176] = _blk(w_down).transpose(0, 2, 1, 3).reshape(128, 176, 128)
    return out.reshape(128, NB * 128)


def _vec8(v):
    return np.ascontiguousarray(np.asarray(v, np.float32).reshape(8, 128).T)


class _Eng:
    def __init__(self, name, skip_self=False):
        self.name = name
        self.semkey = "prog_" + name
        self.count = 0
        self.waited = {}
        self.prog = []
        self.skip_self = skip_self


class _Tracker:
    def __init__(self):
        self.last_w = {}
        self.readers = {}
        self.dma_counts = {}

    def _deps(self, eng, reads, writes):
        evs = {}

        def add(s, v):
            if evs.get(s, 0) < v:
                evs[s] = v
        for k in reads:
            ev = self.last_w.get(k)
            assert ev is not None, "read of never-written key %r" % (k,)
            add(*ev)
        for k in writes:
            ev = self.last_w.get(k)
            if ev is not None:
                add(*ev)
            for s, v in self.readers.get(k, {}).items():
                add(s, v)
        out = []
        for s, v in evs.items():
            if eng.skip_self and s == eng.semkey:
                continue
            if eng.waited.get(s, 0) >= v:
                continue
            eng.waited[s] = v
            out.append((s, v))
        return out

    def _commit(self, ev, reads, writes):
        for k in writes:
            self.last_w[k] = ev
            self.readers[k] = {}
        for k in reads:
            r = self.readers.setdefault(k, {})
            if r.get(ev[0], 0) < ev[1]:
                r[ev[0]] = ev[1]

    def emit(self, eng, fn, reads=(), writes=()):
        waits = self._deps(eng, reads, writes)
        eng.count += 1
        ev = (eng.semkey, eng.count)
        eng.prog.append((waits, fn, eng.semkey, 1))
        self._commit(ev, reads, writes)
        return ev

    def emit_dma(self, eng, fn, semkey, reads=(), writes=()):
        waits = self._deps(eng, reads, writes)
        c = self.dma_counts.get(semkey, 0) + 16
        self.dma_counts[semkey] = c
        ev = (semkey, c)
        eng.prog.append((waits, fn, semkey, 16))
        self._commit(ev, reads, writes)
        return ev

    def emit_dma_batch(self, eng, fns, semkey, reads=(), writes=()):
        waits = self._deps(eng, reads, writes)
        waits = [(s_, v_) for (s_, v_) in waits if s_ != semkey]
        c = self.dma_counts.get(semkey, 0) + 16 * len(fns)
        self.dma_counts[semkey] = c
        ev = (semkey, c)
        for i, fn in enumerate(fns):
            eng.prog.append((waits if i == 0 else [], fn, semkey, 16))
        self._commit(ev, reads, writes)
        return ev

    def emit_wait_only(self, eng, reads):
        waits = self._deps(eng, reads, ())
        eng.prog.append((waits, None, None, 0))


def build_program(n_layers, nseq, seq, final_norm):
    ntok = nseq * seq
    tiles_per_seq = seq // T
    ntiles = nseq * tiles_per_seq
    L = n_layers

    nc = bass.Bass("TRN2", target_bir_lowering=False)
    xin = nc.dram_tensor("xin", [128, KC, ntok], F32, kind="ExternalInput").ap()
    wall = nc.dram_tensor("wall", [L, 128, NB * 128], F32, kind="ExternalInput").ap()
    pvec = nc.dram_tensor("pvec", [128, NPL * L + 8], F32, kind="ExternalInput").ap()
    wst = nc.dram_tensor("wst", [128, L * 8 * 128], F32, kind="ExternalInput").ap()
    bsr = nc.dram_tensor("bsr", [128, L * 8 * 128], F32, kind="ExternalInput").ap()
    yout = nc.dram_tensor("yout", [128, KC, ntok], F32, kind="ExternalOutput").ap()
    wbf = nc.dram_tensor("wbf", [L, 128, NB * 128], BF16, kind="Internal").ap()
    cdr = nc.dram_tensor("cdr", [4, 128, TW + 2], BF16, kind="Internal").ap()
    xs = [nc.dram_tensor("xs%d" % i, [128, KC, ntok], F32, kind="Internal").ap() for i in range(max(L - 1, 0))]

    tr = _Tracker()
    PE = _Eng("pe", skip_self=True)
    ACT = _Eng("act")
    DVE = _Eng("dve")
    POOL = _Eng("pool")
    SP = _Eng("sp", skip_self=True)
    engines = [PE, ACT, DVE, POOL, SP]

    with ExitStack() as es:
        def sb(name, shape, dt):
            return es.enter_context(nc.sbuf_tensor(name, shape, dt))

        xt = [sb("xt%d" % b, [128, KC, TW], F32) for b in range(2)]
        sq = sb("sq", [128, KC, TW], BF16)
        h = sb("h", [128, KC, TW], BF16)
        h2 = sb("h2", [128, KC, T], BF16)
        rstd = sb("rstd", [128, TW], F32)
        c_sb = sb("c_sb", [128, NCS, TW + 2], BF16)
        cst = sb("cst", [128, NCST, 4, CSW], BF16)
        y_sb = sb("y_sb", [128, KC, T], BF16)
        vz = sb("vz", [128, 4 * D], BF16)
        vhat = vz[:, :].rearrange("p (a b) -> p a b", b=D)
        z_sb = vz[:, :].rearrange("p (a b) -> p a b", b=T)
        a_sb = sb("a_sb", [128, NJ, T], BF16)
        vst = sb("vst", [128, 4, 2, 6], F32)
        vmv = sb("vmv", [128, 4, 2], F32)
        vsc = sb("vsc", [128, 4, 2], F32)
        u_sb = sb("u_sb", [128, KC, T], BF16)
        sgth = sb("sgth", [128, 2, 32], F32)
        siga = sb("siga", [128, KC, T], BF16)
        sigb = sb("sigb", [128, KC, T], BF16)
        st_mean = sb("st_mean", [128, T], F32)
        st_rstd = sb("st_rstd", [128, T], F32)
        st_mr = sb("st_mr", [128, T], F32)
        NTMP = 3
        tmp = sb("tmp", [128, NTMP, T], F32)
        gm = u_sb
        wring = sb("wring", [128, RING, CHB * 128], BF16)
        ones = sb("ones", [128, 128], BF16)
        epst = sb("epst", [128, 1], F32)
        pv = sb("pv", [128, NPL * L + 8], F32)
        wst_f = tmp[:, 0:2, :].rearrange("p a b -> p (a b)")
        bsr_v = xt[1][:, 0:2, 0:T]
        wst_b = sb("wst_b", [128, L * 8 * 128], BF16)
        Bsp = sb("Bsp", [128, L * 8, 128], F32)
        ps = es.enter_context(nc.psum_tensor("ps", [128, 8, 512], F32))

        NRING_PS = 6
        state = {"psb": 0, "tmp": 0, "halo": 0, "wfill": 0}

        def next_bank():
            b = state["psb"]
            state["psb"] = (b + 1) % NRING_PS
            return b

        def next_tmp():
            t = state["tmp"]
            state["tmp"] = (t + 1) % NTMP
            return t

        def next_halo():
            s = state["halo"]
            state["halo"] = (s + 1) % 2
            return 6 + s

        def pcol(li, off, m):
            c = li * NPL + off + m
            return pv[:, c:c + 1]

        tr.emit(DVE, lambda e: e.memset(ones[:], 1.0), writes=["ones"])
        tr.emit(DVE, lambda e: e.memset(epst[:], EPS), writes=["eps"])
        tr.emit_dma(SP, lambda e: e.dma_start(out=pv[:], in_=pvec[:, :]), "sem_pv", writes=["pv"])

        seqn = [(li, ti) for li in range(L) for ti in range(ntiles)]

        def emit_cast(li, c):
            b0 = c * CPB
            b1 = min(NB, b0 + CPB)

            def f(e):
                src = wall[li, :, b0 * 128:b1 * 128].rearrange("p (a b) -> p a b", b=1024)
                dst = wbf[li, :, b0 * 128:b1 * 128].rearrange("p (a b) -> p a b", b=1024)
                return e.dma_start(out=dst, in_=src)
            rd = [("wbf", li, c - CWIN)] if c >= CWIN else []
            if li > 0:
                rd = rd + [("h", 0)]
            tr.emit_dma(POOL, f, "sem_c%d_%d" % (li, c % CSEM), reads=rd, writes=[("wbf", li, c)])
        cast0 = {"next": 0}

        def cast0_upto(n):
            while cast0["next"] < min(n, NPIECE):
                emit_cast(0, cast0["next"])
                cast0["next"] += 1
        tr.emit(DVE, lambda e: e.memset(c_sb[:, :, TW:TW + 2], 0.0), writes=[("cpad",)])

        for li in range(L):
            tr.emit_dma(SP, lambda e, li=li: e.dma_start(out=wst_f[:], in_=wst[:, li * 1024:(li + 1) * 1024]),
                        "sem_wst", writes=["wst_f"])
            tr.emit_dma(SP, lambda e, li=li: e.dma_start(
                out=bsr_v, in_=bsr[:, li * 1024:(li + 1) * 1024].rearrange("p (a b) -> p a b", b=T)),
                "sem_bsr", writes=["bsr_sb", ("xt", 1, 0), ("xt", 1, 1)])
            tr.emit(DVE, lambda e, li=li: e.tensor_copy(out=wst_b[:, li * 1024:(li + 1) * 1024], in_=wst_f[:]),
                    reads=["wst_f"], writes=[("wst_b", li)])
            for hf in range(2):
                bank = next_bank()

                def fpe(e, li=li, hf=hf, bank=bank):
                    return e.matmul(ps[:, bank, :], ones[:],
                                    wst_b[:, li * 1024 + hf * 512: li * 1024 + (hf + 1) * 512],
                                    start=True, stop=True)
                tr.emit(PE, fpe, reads=["ones", ("wst_b", li)], writes=[("ps", bank)])
                for gg in range(4):
                    g = hf * 4 + gg

                    def fd(e, li=li, g=g, gg=gg, bank=bank):
                        return e.scalar_tensor_tensor(
                            out=Bsp[:, li * 8 + g, :], in0=ps[:, bank, gg * 128:(gg + 1) * 128],
                            scalar=pcol(li, 56, g), in1=bsr_v[:, g // 4, (g % 4) * 128:(g % 4 + 1) * 128],
                            op0=ALU.mult, op1=ALU.add)
                    tr.emit(DVE, fd, reads=[("ps", bank), "pv", "bsr_sb", ("xt", 1, 0), ("xt", 1, 1)],
                            writes=[("Bsp", li, g)])

        def src_dram(li):
            return xin if li == 0 else xs[li - 1]

        def dst_dram(li):
            return yout if li == L - 1 else xs[li]

        def xkeys(b):
            return [("xt", b, kc) for kc in range(KC)]

        def xd_keys(li, j):
            return [("xd", li, j, kc) for kc in range(KC)]

        def load_x(li, ti):
            b = (li * ntiles + ti) % 2
            sidx, tt = divmod(ti, tiles_per_seq)
            t0 = sidx * seq + tt * T
            src = src_dram(li)
            rdh = [k for j in (ti - 1, ti, ti + 1) if 0 <= j < ntiles for k in xd_keys(li, j)] if li > 0 else []
            for kc in range(KC):
                rd = [("xd", li, ti, kc)] if li > 0 else []
                tr.emit_dma(SP, lambda e, kc=kc: e.dma_start(out=xt[b][:, kc, 0:T], in_=src[:, kc, t0:t0 + T]),
                            "sem_xm%d_%d" % (b, kc), reads=rd, writes=[("xt", b, kc)])
            if tt > 0:
                tr.emit_dma(SP, lambda e: e.dma_start(out=xt[b][:, :, T:T + HALO], in_=src[:, :, t0 - HALO:t0]),
                            "sem_xl%d" % b, reads=rdh, writes=[("xth", b, 0)])
            else:
                tr.emit(DVE, lambda e: e.memset(xt[b][:, :, T:T + HALO], 0.0), writes=[("xth", b, 0)])
            if tt < tiles_per_seq - 1:
                tr.emit_dma(SP, lambda e: e.dma_start(out=xt[b][:, :, T + HALO:TW], in_=src[:, :, t0 + T:t0 + T + HALO]),
                            "sem_xr%d" % b, reads=rdh, writes=[("xth", b, 1)])
            else:
                tr.emit(DVE, lambda e: e.memset(xt[b][:, :, T + HALO:TW], 0.0), writes=[("xth", b, 1)])

        def store_x_chunk(li, ti, kc):
            b = (li * ntiles + ti) % 2
            sidx, tt = divmod(ti, tiles_per_seq)
            t0 = sidx * seq + tt * T
            dst = dst_dram(li)
            tr.emit_dma(SP, lambda e: e.dma_start(out=dst[:, kc, t0:t0 + T], in_=xt[b][:, kc, 0:T]),
                        "sem_st%d_%d" % (b, kc), reads=[("xt", b, kc)], writes=[("xd", li + 1, ti, kc)])

        SEGS = {"A1": (0, OFF_MIX), "A3": (OFF_MIX, OFF_FFN),
                "B1": (OFF_FFN, OFF_DOWN), "B2": (OFF_DOWN, NB)}
        NIT = len(seqn) + 1
        chunks = []
        seg_base = {}
        for it in range(NIT):
            for seg in ("A1", "B1", "B2", "A3"):
                tl = it if seg[0] == "A" else it - 1
                if tl < 0 or tl >= len(seqn):
                    continue
                seg_base[(it, seg)] = len(chunks)
                b0, b1 = SEGS[seg]
                for blk in range(b0, b1, CHB):
                    chunks.append((seqn[tl][0], blk, min(CHB, b1 - blk)))
        cur = {"it": 0, "in_v": False}
        state["wnext"] = 0

        def issue_wchunk(G):
            li_, blk0, nb = chunks[G]
            s = G % RING
            rd = [("wbf", li_, cc) for cc in range(blk0 // CPB, (blk0 + nb - 1) // CPB + 1)]
            tr.emit_dma(SP, lambda e: e.dma_start(out=wring[:, s, 0:nb * 128],
                                                  in_=wbf[li_, :, blk0 * 128:(blk0 + nb) * 128]),
                        "sem_w%d" % s, reads=rd, writes=[("wslot", s)])

        def touch(gset):
            gmin = seg_base[(cur["it"], "A1")] if cur["in_v"] else min(gset)
            assert max(gset) < gmin + RING
            limit = min(gmin + RING, len(chunks))
            while state["wnext"] < limit:
                issue_wchunk(state["wnext"])
                state["wnext"] += 1

        def wblk(bidx, n=1):
            for seg, (b0, b1) in SEGS.items():
                if b0 <= bidx < b1:
                    break
            c, o = divmod(bidx - b0, CHB)
            assert o + n <= CHB
            G = seg_base[(cur["it"], seg)] + c
            s = G % RING
            return wring[:, s, o * 128:(o + n) * 128], ("wslot", s), G

        def rms_squares(b, width_halo):
            wd = TW if width_halo else T
            for kc in range(KC):
                rd = [("xt", b, kc)] + ([("xth", b, 0), ("xth", b, 1)] if width_halo else [])
                tr.emit(ACT, lambda e, kc=kc: e.activation(out=sq[:, kc, 0:wd], in_=xt[b][:, kc, 0:wd], func=AF.Square),
                        reads=rd, writes=[("sq", kc)])

        def rms_rstd(width_halo):
            bank = next_bank()

            def fpe(e):
                for kc in range(KC):
                    ins = e.matmul(ps[:, bank, :], ones[:], sq[:, kc, 0:T], start=(kc == 0), stop=(kc == KC - 1))
                return ins
            tr.emit(PE, fpe, reads=["ones"] + [("sq", kc) for kc in range(KC)], writes=[("ps", bank)])
            tr.emit(ACT, lambda e: e.activation(out=rstd[:, 0:T], in_=ps[:, bank, :], func=AF.Sqrt,
                                                bias=epst[:, 0:1], scale=1.0 / D),
                    reads=[("ps", bank), "eps"], writes=["rstd_m"])
            if width_halo:
                hs = next_halo()

                def fpe2(e):
                    for kc in range(KC):
                        ins = e.matmul(ps[:, hs, 0:2 * HALO], ones[:], sq[:, kc, T:TW],
                                       start=(kc == 0), stop=(kc == KC - 1))
                    return ins
                tr.emit(PE, fpe2, reads=["ones"] + [("sq", kc) for kc in range(KC)], writes=[("ps", hs)])
                tr.emit(ACT, lambda e: e.activation(out=rstd[:, T:TW], in_=ps[:, hs, 0:2 * HALO],
                                                    func=AF.Sqrt, bias=epst[:, 0:1], scale=1.0 / D),
                        reads=[("ps", hs), "eps"], writes=["rstd_h"])
                tr.emit(DVE, lambda e: e.reciprocal(out=rstd[:, 0:TW], in_=rstd[:, 0:TW]),
                        reads=["rstd_m", "rstd_h"], writes=["rstd_m", "rstd_h"])
            else:
                tr.emit(DVE, lambda e: e.reciprocal(out=rstd[:, 0:T], in_=rstd[:, 0:T]),
                        reads=["rstd_m"], writes=["rstd_m"])

        def rms_stats(b, width_halo):
            rms_squares(b, width_halo)
            rms_rstd(width_halo)

        def norm_to_h_one(b, li, goff, width_halo, kc, hbuf, hname):
            wd = TW if width_halo else T
            rd = [("xt", b, kc), "rstd_m", "pv"] + ([("xth", b, 0), ("xth", b, 1), "rstd_h"] if width_halo else [])
            tr.emit(DVE, lambda e: e.scalar_tensor_tensor(
                out=hbuf[:, kc, 0:wd], in0=xt[b][:, kc, 0:wd], scalar=pcol(li, goff, kc), in1=rstd[:, 0:wd],
                op0=ALU.mult, op1=ALU.mult), reads=rd, writes=[(hname, kc)])

        def norm_to_h(b, li, goff, width_halo, hbuf, hname):
            for kc in range(KC):
                norm_to_h_one(b, li, goff, width_halo, kc, hbuf, hname)

        def hkeys(hname="h"):
            return [(hname, kc) for kc in range(KC)]

        def mm_group(bank_ap, blocks0, rhs_fn, nk, reads, wkey_out, stride=1):
            aps = []
            wkeys = set()
            chunks = set()
            for k in range(nk):
                ap, key, c = wblk(blocks0 + k * stride)
                aps.append(ap)
                wkeys.add(key)
                chunks.add(c)
            touch(chunks)

            def fpe(e):
                for k in range(nk):
                    ins = e.matmul(bank_ap, aps[k], rhs_fn(k), start=(k == 0), stop=(k == nk - 1))
                return ins
            tr.emit(PE, fpe, reads=list(wkeys) + list(reads), writes=[wkey_out])

        def stage_proj(b, li):
            for tc in range(4):
                banks = []
                for hf in range(2):
                    bank = next_bank()
                    banks.append(bank)
                    aps = []
                    wkeys = set()
                    chunks = set()
                    for k in range(KC):
                        ap, key, c = wblk(OFF_V + hf * 32 + k * 4, 4)
                        aps.append(ap)
                        wkeys.add(key)
                        chunks.add(c)
                    cur["in_v"] = True
                    touch(chunks)
                    cur["in_v"] = False

                    def fpe(e, tc=tc, bank=bank, aps=aps):
                        for k in range(KC):
                            ins = e.matmul(ps[:, bank, :], h[:, k, tc * 128:(tc + 1) * 128], aps[k],
                                           start=(k == 0), stop=(k == KC - 1))
                        return ins
                    tr.emit(PE, fpe, reads=list(wkeys) + hkeys(), writes=[("ps", bank)])
                    tr.emit(DVE, lambda e, tc=tc, hf=hf, bank=bank: e.bn_stats(out=vst[:, tc, hf, :], in_=ps[:, bank, :]),
                            reads=[("ps", bank)], writes=[("vst", tc, hf)])
                tr.emit(DVE, lambda e, tc=tc: e.bn_aggr(out=vmv[:, tc, :], in_=vst[:, tc, :, :].rearrange("p a b -> p (a b)")),
                        reads=[("vst", tc, 0), ("vst", tc, 1)], writes=[("vmv", tc)])
                tr.emit(ACT, lambda e, tc=tc: e.activation(out=vsc[:, tc, 0:1], in_=vmv[:, tc, 1:2], func=AF.Sqrt,
                                                           bias=epst[:, 0:1], scale=1.0),
                        reads=[("vmv", tc), "eps"], writes=[("vsc0", tc)])
                tr.emit(DVE, lambda e, tc=tc: e.reciprocal(out=vsc[:, tc, 0:1], in_=vsc[:, tc, 0:1]),
                        reads=[("vsc0", tc)], writes=[("vsc0", tc)])
                tr.emit(DVE, lambda e, tc=tc: e.scalar_tensor_tensor(
                    out=vsc[:, tc, 1:2], in0=vmv[:, tc, 0:1], scalar=-1.0, in1=vsc[:, tc, 0:1],
                    op0=ALU.mult, op1=ALU.mult), reads=[("vmv", tc), ("vsc0", tc)], writes=[("vsc1", tc)])
                for hf in range(2):
                    tr.emit(ACT, lambda e, tc=tc, hf=hf, bank=banks[hf]: e.activation(
                        out=vhat[:, tc, hf * 512:(hf + 1) * 512], in_=ps[:, bank, :], func=AF.Identity,
                        bias=vsc[:, tc, 1:2], scale=vsc[:, tc, 0:1]),
                        reads=[("ps", banks[hf]), ("vsc0", tc), ("vsc1", tc)], writes=[("vhat", tc, hf)])
            for m in range(KC):
                bank = next_bank()
                mm_group(ps[:, bank, :], OFF_U + m * 8, lambda k: h[:, k, 0:T], KC, hkeys(), ("ps", bank))
                tr.emit(ACT, lambda e, m=m, bank=bank: e.activation(out=u_sb[:, m, :], in_=ps[:, bank, :], func=AF.Copy),
                        reads=[("ps", bank)], writes=[("u", m)])
            def gv_pos(m):
                return OFF_GV + 16 * m + 8 * max(0, m - 3)

            def cu_pos(m):
                return gv_pos(m + 3) + 16 if m <= 4 else OFF_CU5 + (m - 5) * 8

            def conv_chunk(m):
                slot = m % NCST
                bank = next_bank()
                ap, key, G = wblk(cu_pos(m), 8)
                touch({G})

                def fpe(e):
                    for q in range(8):
                        for j in range(4):
                            ins = e.matmul(ps[32 * j:32 * j + 32, bank, :], ap[:, (q * 4 + j) * 32:(q * 4 + j + 1) * 32],
                                           cst[:, slot, j, 4 * q:4 * q + T], start=(q == 0), stop=(q == 7),
                                           tile_position=(0, 32 * j))
                    return ins
                tr.emit(PE, fpe, reads=[key, ("cst", slot)], writes=[("ps", bank)])
                tr.emit(ACT, lambda e: e.activation(out=y_sb[:, m, :], in_=ps[:, bank, :], func=AF.Identity,
                                                    bias=pcol(li, 24, m), scale=1.0),
                        reads=[("ps", bank), "pv"], writes=[("y", m)])
                tr.emit(ACT, lambda e: e.activation(out=sq[:, m, 0:T], in_=ps[:, bank, :], func=AF.Square,
                                                    bias=pcol(li, 24, m), scale=1.0),
                        reads=[("ps", bank), "pv"], writes=[("sq", m)])

            for m in range(KC):
                sl = m % 2
                cs_ = m % NCS
                bank_g = next_bank()
                hs_g = next_halo()
                mm_group(ps[:, bank_g, :], gv_pos(m), lambda k: h[:, k, 0:T], KC, hkeys(), ("ps", bank_g))
                mm_group(ps[:, hs_g, 0:2 * HALO], gv_pos(m), lambda k: h[:, k, T:TW], KC,
                         hkeys(), ("ps", hs_g))
                ts_ = next_tmp()
                tk = [("tmp", ts_, tc) for tc in range(4)]
                tr.emit(ACT, lambda e, ts_=ts_, bank=bank_g: e.activation(out=tmp[:, ts_, :], in_=ps[:, bank, :],
                                                                           func=AF.Sigmoid),
                        reads=[("ps", bank_g)], writes=tk)
                tr.emit(ACT, lambda e, sl=sl, hs=hs_g: e.activation(out=sgth[:, sl, 0:2 * HALO], in_=ps[:, hs, 0:2 * HALO],
                                                                     func=AF.Sigmoid),
                        reads=[("ps", hs_g)], writes=[("sgth", sl)])
                bank_v = next_bank()
                hs_v = next_halo()
                mm_group(ps[:, bank_v, :], gv_pos(m) + 8, lambda k: h[:, k, 0:T], KC, hkeys(), ("ps", bank_v))
                mm_group(ps[:, hs_v, 0:2 * HALO], gv_pos(m) + 8, lambda k: h[:, k, T:TW], KC,
                         hkeys(), ("ps", hs_v))
                tr.emit(DVE, lambda e, cs_=cs_, ts_=ts_, bank=bank_v: e.tensor_tensor(
                    out=c_sb[:, cs_, HALO:HALO + T], in0=ps[:, bank, :], in1=tmp[:, ts_, :], op=ALU.mult),
                    reads=[("ps", bank_v)] + tk, writes=[("c", cs_)])
                tr.emit(DVE, lambda e, cs_=cs_, sl=sl, hs=hs_v: e.tensor_tensor(
                    out=c_sb[:, cs_, 0:HALO], in0=ps[:, hs, 0:HALO], in1=sgth[:, sl, 0:HALO], op=ALU.mult),
                    reads=[("ps", hs_v), ("sgth", sl)], writes=[("ch0", cs_)])
                tr.emit(DVE, lambda e, cs_=cs_, sl=sl, hs=hs_v: e.tensor_tensor(
                    out=c_sb[:, cs_, HALO + T:TW], in0=ps[:, hs, HALO:2 * HALO],
                    in1=sgth[:, sl, HALO:2 * HALO], op=ALU.mult),
                    reads=[("ps", hs_v), ("sgth", sl)], writes=[("ch1", cs_)])
                slot = m % NCST
                fns = []
                ds = m % 4
                tr.emit_dma(SP, lambda e, cs_=cs_, ds=ds: e.dma_start(out=cdr[ds, :, :], in_=c_sb[:, cs_, :]),
                            "sem_cdw%d" % ds, reads=[("c", cs_), ("ch0", cs_), ("ch1", cs_), ("cpad",)],
                            writes=[("cdr", ds)])
                for s_ in range(4):
                    fns.append(lambda e, s_=s_, ds=ds, slot=slot: e.dma_start(
                        out=cst[32 * s_:32 * s_ + 32, slot, :, :],
                        in_=cdr[ds, :, s_:s_ + CSW].rearrange("(j c) x -> c j x", j=4)))
                tr.emit_dma_batch(SP, fns, "sem_cst%d" % slot, reads=[("cdr", ds)], writes=[("cst", slot)])
                if m >= 3:
                    conv_chunk(m - 3)
            for m in range(KC):
                for which, dstt, boff in ((0, siga, 8), (1, sigb, 16)):
                    bank = next_bank()
                    mm_group(ps[:, bank, :], OFF_GAB + m * 16 + which * 8, lambda k: h[:, k, 0:T], KC, hkeys(), ("ps", bank))
                    tr.emit(ACT, lambda e, m=m, bank=bank, dstt=dstt, boff=boff: e.activation(
                        out=dstt[:, m, :], in_=ps[:, bank, :], func=AF.Sigmoid, bias=pcol(li, boff, m), scale=1.0),
                        reads=[("ps", bank), "pv"], writes=[("sig", which, m)])
            for m in (5, 6, 7):
                conv_chunk(m)

        def stage_spatial(li):
            for g in range(KC):
                bank = next_bank()

                def fpe(e, g=g, bank=bank):
                    for tc in range(4):
                        ins = e.matmul(ps[:, bank, tc * 128:(tc + 1) * 128], vhat[:, tc, g * 128:(g + 1) * 128],
                                       wst_b[:, li * 1024 + g * 128: li * 1024 + (g + 1) * 128], start=True, stop=True)
                    return ins
                tr.emit(PE, fpe, reads=[("vhat", tc, g // 4) for tc in range(4)] + [("wst_b", li)], writes=[("ps", bank)])
                tslot = next_tmp()
                for tc in range(4):
                    tr.emit(DVE, lambda e, g=g, tc=tc, bank=bank, tslot=tslot: e.scalar_tensor_tensor(
                        out=tmp[:, tslot, tc * 128:(tc + 1) * 128], in0=ps[:, bank, tc * 128:(tc + 1) * 128],
                        scalar=pcol(li, 48, g), in1=Bsp[:, li * 8 + g, :], op0=ALU.mult, op1=ALU.add),
                        reads=[("ps", bank), "pv", ("Bsp", li, g)], writes=[("tmp", tslot, tc)])
                tr.emit(DVE, lambda e, g=g, tslot=tslot: e.tensor_tensor(out=gm[:, g, :], in0=tmp[:, tslot, :],
                                                                           in1=u_sb[:, g, :], op=ALU.mult),
                        reads=[("tmp", tslot, tc) for tc in range(4)] + [("u", g)], writes=[("u", g)])

        def stage_conv_ln(li):
            b1 = next_bank()
            b2 = next_bank()

            def fpe1(e):
                for kc in range(KC):
                    ins = e.matmul(ps[:, b1, :], ones[:], y_sb[:, kc, :], start=(kc == 0), stop=(kc == KC - 1))
                return ins

            def fpe2(e):
                for kc in range(KC):
                    ins = e.matmul(ps[:, b2, :], ones[:], sq[:, kc, 0:T], start=(kc == 0), stop=(kc == KC - 1))
                return ins
            tr.emit(PE, fpe1, reads=["ones"] + [("y", m) for m in range(KC)], writes=[("ps", b1)])
            tr.emit(PE, fpe2, reads=["ones"] + [("sq", m) for m in range(KC)], writes=[("ps", b2)])
            tr.emit(DVE, lambda e: e.tensor_scalar(out=st_mean[:], in0=ps[:, b1, :], scalar1=1.0 / D, scalar2=None,
                                                   op0=ALU.mult), reads=[("ps", b1)], writes=["st_mean"])
            tr.emit(DVE, lambda e: e.tensor_tensor(out=st_mr[:], in0=st_mean[:], in1=st_mean[:], op=ALU.mult),
                    reads=["st_mean"], writes=["st_mr"])
            tr.emit(DVE, lambda e: e.scalar_tensor_tensor(out=st_rstd[:], in0=ps[:, b2, :], scalar=1.0 / D, in1=st_mr[:],
                                                          op0=ALU.mult, op1=ALU.subtract),
                    reads=[("ps", b2), "st_mr"], writes=["st_rstd"])
            tr.emit(ACT, lambda e: e.activation(out=st_rstd[:], in_=st_rstd[:], func=AF.Sqrt, bias=epst[:, 0:1], scale=1.0),
                    reads=["st_rstd", "eps"], writes=["st_rstd"])
            tr.emit(DVE, lambda e: e.reciprocal(out=st_rstd[:], in_=st_rstd[:]), reads=["st_rstd"], writes=["st_rstd"])
            tr.emit(DVE, lambda e: e.tensor_tensor(out=st_mr[:], in0=st_mean[:], in1=st_rstd[:], op=ALU.mult),
                    reads=["st_mean", "st_rstd"], writes=["st_mr"])
            for m in range(KC):
                tslot = next_tmp()
                tk = [("tmp", tslot, tc) for tc in range(4)]
                tr.emit(DVE, lambda e, m=m, tslot=tslot: e.tensor_tensor(out=tmp[:, tslot, :], in0=y_sb[:, m, :],
                                                                           in1=st_rstd[:], op=ALU.mult),
                        reads=[("y", m), "st_rstd"], writes=tk)
                tr.emit(DVE, lambda e, tslot=tslot: e.tensor_tensor(out=tmp[:, tslot, :], in0=tmp[:, tslot, :],
                                                                      in1=st_mr[:], op=ALU.subtract),
                        reads=tk + ["st_mr"], writes=tk)
                tr.emit(ACT, lambda e, m=m, tslot=tslot: e.activation(out=z_sb[:, m, :], in_=tmp[:, tslot, :], func=AF.Silu,
                                                                       bias=pcol(li, 40, m), scale=pcol(li, 32, m)),
                        reads=tk + ["pv"], writes=[("z", m)])

        def stage_mix_pairs(li):
            gk = [("u", k) for k in range(KC)]
            zk = [("z", k) for k in range(KC)]
            for m in range(KC):
                bank_b = next_bank()
                mm_group(ps[:, bank_b, :], OFF_MIX + m * 16, lambda k: gm[:, k, :], KC, gk, ("ps", bank_b))
                bank_a = next_bank()
                mm_group(ps[:, bank_a, :], OFF_MIX + m * 16 + 8, lambda k: z_sb[:, k, :], KC, zk, ("ps", bank_a))
                t1 = next_tmp()
                t2 = next_tmp()
                k1 = [("tmp", t1, tc) for tc in range(4)]
                k2 = [("tmp", t2, tc) for tc in range(4)]
                tr.emit(DVE, lambda e, m=m, bank=bank_b, t1=t1: e.tensor_tensor(out=tmp[:, t1, :], in0=ps[:, bank, :],
                                                                                 in1=sigb[:, m, :], op=ALU.mult),
                        reads=[("ps", bank_b), ("sig", 1, m)], writes=k1)
                tr.emit(DVE, lambda e, m=m, bank=bank_a, t2=t2: e.tensor_tensor(out=tmp[:, t2, :], in0=ps[:, bank, :],
                                                                                 in1=siga[:, m, :], op=ALU.mult),
                        reads=[("ps", bank_a), ("sig", 0, m)], writes=k2)
                tr.emit(DVE, lambda e, m=m, t1=t1, t2=t2: e.tensor_tensor(out=a_sb[:, m, :], in0=tmp[:, t1, :],
                                                                            in1=tmp[:, t2, :], op=ALU.add),
                        reads=k1 + k2, writes=[("a", m)])

        def stage_wo(b, li, after_each=None):
            mk = [("a", k) for k in range(KC)]
            for m in range(KC):
                bank = next_bank()
                mm_group(ps[:, bank, :], OFF_WO + m * 8, lambda k: a_sb[:, k, :], KC, mk, ("ps", bank))
                tr.emit(DVE, lambda e, m=m, bank=bank: e.tensor_tensor(out=xt[b][:, m, 0:T], in0=ps[:, bank, :],
                                                                        in1=xt[b][:, m, 0:T], op=ALU.add),
                        reads=[("ps", bank), ("xt", b, m)], writes=[("xt", b, m)])
                if after_each is not None:
                    after_each(m)

        def stage_ffn_up(li):
            for j in range(NJ):
                bank_g = next_bank()
                mm_group(ps[:, bank_g, :], OFF_FFN + j * 16, lambda k: h2[:, k, :], KC, hkeys("h2"), ("ps", bank_g))
                bank_u = next_bank()
                mm_group(ps[:, bank_u, :], OFF_FFN + j * 16 + 8, lambda k: h2[:, k, :], KC, hkeys("h2"), ("ps", bank_u))
                tslot = next_tmp()
                tk = [("tmp", tslot, tc) for tc in range(4)]
                tr.emit(ACT, lambda e, bank=bank_g, tslot=tslot: e.activation(out=tmp[:, tslot, :], in_=ps[:, bank, :],
                                                                               func=AF.Silu),
                        reads=[("ps", bank_g)], writes=tk)
                tr.emit(DVE, lambda e, j=j, bank=bank_u, tslot=tslot: e.tensor_tensor(
                    out=a_sb[:, j, :], in0=ps[:, bank, :], in1=tmp[:, tslot, :], op=ALU.mult),
                    reads=[("ps", bank_u)] + tk, writes=[("a", j)])

        def stage_ffn_down(b, li, after_each=None):
            ak = [("a", j) for j in range(NJ)]
            for m in range(KC):
                bank = next_bank()
                mm_group(ps[:, bank, :], OFF_DOWN + m * NJ, lambda k: a_sb[:, k, :], NJ, ak, ("ps", bank))
                tr.emit(DVE, lambda e, m=m, bank=bank: e.tensor_tensor(out=xt[b][:, m, 0:T], in0=ps[:, bank, :],
                                                                        in1=xt[b][:, m, 0:T], op=ALU.add),
                        reads=[("ps", bank), ("xt", b, m)], writes=[("xt", b, m)])
                if after_each is not None:
                    after_each(m)

        def stage_final_norm(b, after_each=None):
            rms_stats(b, False)
            for kc in range(KC):
                c = L * NPL + kc
                tr.emit(DVE, lambda e, kc=kc, c=c: e.scalar_tensor_tensor(
                    out=xt[b][:, kc, 0:T], in0=xt[b][:, kc, 0:T], scalar=pv[:, c:c + 1], in1=rstd[:, 0:T],
                    op0=ALU.mult, op1=ALU.mult), reads=[("xt", b, kc), "rstd_m", "pv"], writes=[("xt", b, kc)])
                if after_each is not None:
                    after_each(kc)

        NTL = len(seqn)

        def emit_casts_for(n):
            li, ti = seqn[n]
            if li + 1 < L:
                per = (NPIECE + ntiles - 1) // ntiles
                for c in range(ti * per, min((ti + 1) * per, NPIECE)):
                    emit_cast(li + 1, c)

        load_x(*seqn[0])
        cast0_upto(NPIECE)
        rms_stats(0, True)
        norm_to_h(0, seqn[0][0], 0, True, h, "h")
        for it in range(NTL + 1):
            cur["it"] = it
            hasA = it < NTL
            hasB = it >= 1
            if hasA:
                li, ti = seqn[it]
                b = it % 2
                emit_casts_for(it)
            if hasB:
                lb, tb = seqn[it - 1]
                bb = (it - 1) % 2
            if hasA:
                stage_proj(b, li)
                if it == 0:
                    cast0_upto(8)
            if hasB:
                stage_ffn_up(lb)
            if hasA:
                stage_spatial(li)
                stage_conv_ln(li)
            if hasB:
                st_hook = lambda kc, lb=lb, tb=tb: store_x_chunk(lb, tb, kc)
                if final_norm and lb == L - 1:
                    stage_ffn_down(bb, lb)
                    stage_final_norm(bb, st_hook)
                else:
                    stage_ffn_down(bb, lb, st_hook)
            if it + 1 < NTL:
                load_x(*seqn[it + 1])
                rms_squares((it + 1) % 2, True)
            if hasA:
                stage_mix_pairs(li)
                if it == 0:
                    cast0_upto(NPIECE)
            if it + 1 < NTL:
                rms_rstd(True)
            if hasA:
                nxt = (it + 1) % 2
                lnx = seqn[it + 1][0] if it + 1 < NTL else None
                hook = (lambda m: norm_to_h_one(nxt, lnx, 0, True, m, h, "h")) if it + 1 < NTL else None
                stage_wo(b, li, hook)
                rms_stats(b, False)
                norm_to_h(b, li, 64, False, h2, "h2")
        tr.emit_wait_only(SP, [k for ti in range(ntiles) for k in xd_keys(L, ti)])

        semkeys = set()
        for eng in engines:
            for waits, fn, sk, inc in eng.prog:
                for s, v in waits:
                    semkeys.add(s)
                if sk:
                    semkeys.add(sk)
        sems = {k: es.enter_context(nc.semaphore(k)) for k in sorted(semkeys)}

        def replay(eng, handle):
            for waits, fn, sk, inc in eng.prog:
                for s, v in waits:
                    handle.wait_ge(sems[s], v)
                if fn is not None:
                    ins = fn(handle)
                    ins.then_inc(sems[sk], inc)

        block = es.enter_context(nc.Block())
        block.tensor(lambda e: replay(PE, e))
        block.scalar(lambda e: replay(ACT, e))
        block.vector(lambda e: replay(DVE, e))
        block.gpsimd(lambda e: replay(POOL, e))
        block.sync(lambda e: replay(SP, e))
    return nc


def _prep_params(inputs, layers, with_final):
    L = len(layers)
    wall = np.stack([_layer_blocks(inputs["w_in"][l], inputs["conv_w"][l], inputs["w_conv_out"][l],
                                   inputs["w_sgu_out"][l], inputs["w_o"][l], inputs["w_ffn_gate"][l],
                                   inputs["w_ffn_up"][l], inputs["w_ffn_down"][l]) for l in layers])
    pvec = np.zeros((128, NPL * L + 8), np.float32)
    wst = np.zeros((128, L * 1024), np.float32)
    bsr = np.zeros((128, L * 1024), np.float32)
    for i, l in enumerate(layers):
        o = i * NPL
        pvec[:, o + 0:o + 8] = _vec8(inputs["norm_mix"][l])
        pvec[:, o + 8:o + 16] = _vec8(inputs["gate_bias"][l][:D])
        pvec[:, o + 16:o + 24] = _vec8(inputs["gate_bias"][l][D:])
        pvec[:, o + 24:o + 32] = _vec8(inputs["conv_b"][l])
        pvec[:, o + 32:o + 40] = _vec8(inputs["conv_ln_g"][l])
        pvec[:, o + 40:o + 48] = _vec8(inputs["conv_ln_b"][l])
        pvec[:, o + 48:o + 56] = _vec8(inputs["sgu_ln_g"][l])
        pvec[:, o + 56:o + 64] = _vec8(inputs["sgu_ln_b"][l])
        pvec[:, o + 64:o + 72] = _vec8(inputs["norm_ffn"][l])
        wst[:, i * 1024:(i + 1) * 1024] = inputs["w_spatial"][l].transpose(2, 0, 1).reshape(128, 1024)
        bsr[:, i * 1024:(i + 1) * 1024] = np.broadcast_to(inputs["b_spatial"][l].reshape(1, 1024), (128, 1024))
    if with_final:
        pvec[:, NPL * L:NPL * L + 8] = _vec8(inputs["norm_final"])
    return wall, pvec, wst, bsr


def _to_fm(x, ncores):
    B, S, _ = x.shape
    nseq = B // ncores
    xr = x.reshape(ncores, nseq * S, 8, 128)
    return np.ascontiguousarray(xr.transpose(0, 3, 2, 1))


def _from_fm(y, B, S):
    ncores = y.shape[0]
    return np.ascontiguousarray(y.transpose(0, 3, 2, 1)).reshape(B, S, D)


_PROG_CACHE = {}


def _get_prog(n_layers, nseq, seq, final_norm):
    key = (n_layers, nseq, seq, final_norm)
    if key not in _PROG_CACHE:
        _PROG_CACHE[key] = build_program(n_layers, nseq, seq, final_norm)
    return _PROG_CACHE[key]


def run_layers(xfm, inputs, layers, with_final, ncores, nseq, seq):
    wall, pvec, wst, bsr = _prep_params(inputs, layers, with_final)
    nc = _get_prog(len(layers), nseq, seq, with_final)
    in_maps = [{"xin": xfm[c], "wall": wall, "pvec": pvec, "wst": wst, "bsr": bsr} for c in range(ncores)]
    res = run_bass_kernel_spmd(nc, in_maps, core_ids=list(range(ncores)))
    return np.stack([np.asarray(r["yout"]) for r in res.results])


FUSED = True


def kernel(**inputs):
    inputs = {k: np.asarray(v, np.float32) for k, v in inputs.items()}
    x = inputs["x"]
    B, S, _ = x.shape
    depth = inputs["w_in"].shape[0]
    nseq = B // NCORES
    xfm = _to_fm(x, NCORES)
    if FUSED:
        y = run_layers(xfm, inputs, list(range(depth)), True, NCORES, nseq, S)
    else:
        y = xfm
        for l in range(depth):
            y = run_layers(y, inputs, [l], l == depth - 1, NCORES, nseq, S)
    return _from_fm(y, B, S).astype(np.float32)
```

```python
import numpy as np
from contextlib import ExitStack
import concourse.bass as bass
import concourse.mybir as mybir
from concourse.bass_utils import run_bass_kernel_spmd

F32 = mybir.dt.float32
BF16 = mybir.dt.bfloat16
AF = mybir.ActivationFunctionType
ALU = mybir.AluOpType

D = 1024
KC = 8
DFF = 2816
NJ = 22
CW = 31
HALO = 15
T = 512
TW = T + 2 * HALO
EPS = 1e-6
NCORES = 8

OFF_V = 0
OFF_U = 64
OFF_GV = 128
GVLEN = 128 + 40
OFF_GAB = OFF_GV + GVLEN
OFF_CU5 = OFF_GAB + 128
OFF_MIX = OFF_CU5 + 24
OFF_WO = OFF_MIX + 128
OFF_FFN = OFF_WO + 64
OFF_DOWN = OFF_FFN + 352
NB = OFF_DOWN + 176
CHB = 32
RING = 5
CPB = 16
CWIN = 3
CSEM = 6
NPIECE = (NB + CPB - 1) // CPB
NCS = 2
NCST = 4
CSW = 540
NPL = 72


def _blk(W):
    K, M = W.shape
    return W.reshape(K // 128, 128, M // 128, 128).transpose(1, 0, 2, 3)


def _layer_blocks(w_in, conv_w, w_conv_out, w_sgu_out, w_o, w_gate, w_up, w_down):
    out = np.zeros((128, NB, 128), np.float32)
    bv = _blk(w_in[:, 3072:4096])
    out[:, OFF_V:OFF_V + 64] = bv.reshape(128, 8, 2, 4, 128).transpose(0, 2, 1, 3, 4).reshape(128, 64, 128)
    out[:, OFF_U:OFF_U + 64] = _blk(w_in[:, 2048:3072]).transpose(0, 2, 1, 3).reshape(128, 64, 128)
    bg = _blk(w_in[:, 1024:2048]).transpose(0, 2, 1, 3)
    bva = _blk(w_in[:, 0:1024]).transpose(0, 2, 1, 3)
    wpad = np.zeros((32, 1024), np.float32)
    wpad[:CW] = conv_w.reshape(CW, 1024)
    w5 = wpad.reshape(8, 4, 8, 4, 32)
    cu = np.zeros((8, 4, 32, 8, 4, 32), np.float32)
    ii = np.arange(32)
    cu[:, :, ii, :, :, ii] = w5.transpose(4, 2, 1, 0, 3)
    cu = cu.reshape(8, 128, 8, 128).transpose(1, 0, 2, 3)
    cu = cu.reshape(128, 8, 1024).reshape(128, 8, 8, 128)
    pos = OFF_GV
    for m in range(8):
        out[:, pos:pos + 8] = bg[:, m]
        out[:, pos + 8:pos + 16] = bva[:, m]
        pos += 16
        if m >= 3:
            out[:, pos:pos + 8] = cu[:, m - 3]
            pos += 8
    assert pos == OFF_GAB
    for m in (5, 6, 7):
        out[:, OFF_CU5 + (m - 5) * 8:OFF_CU5 + (m - 4) * 8] = cu[:, m]
    ba = _blk(w_in[:, 4096:5120]).transpose(0, 2, 1, 3)
    bb = _blk(w_in[:, 5120:6144]).transpose(0, 2, 1, 3)
    out[:, OFF_GAB:OFF_GAB + 128] = np.stack([ba, bb], axis=2).reshape(128, 128, 128)
    bs = _blk(w_sgu_out).transpose(0, 2, 1, 3)
    bc = _blk(w_conv_out).transpose(0, 2, 1, 3)
    out[:, OFF_MIX:OFF_MIX + 128] = np.stack([bs, bc], axis=2).reshape(128, 128, 128)
    out[:, OFF_WO:OFF_WO + 64] = _blk(w_o).transpose(0, 2, 1, 3).reshape(128, 64, 128)
    fg = _blk(w_gate).transpose(0, 2, 1, 3)
    fu = _blk(w_up).transpose(0, 2, 1, 3)
    out[:, OFF_FFN:OFF_FFN + 352] = np.stack([fg, fu], axis=2).reshape(128, 352, 128)
    out[:, OFF_DOWN:OFF_DOWN + 176] = _blk(w_down).transpose(0, 2, 1, 3).reshape(128, 176, 128)
    return out.reshape(128, NB * 128)


def _vec8(v):
    return np.ascontiguousarray(np.asarray(v, np.float32).reshape(8, 128).T)


class _Eng:
    def __init__(self, name, skip_self=False):
        self.name = name
        self.semkey = "prog_" + name
        self.count = 0
        self.waited = {}
        self.prog = []
        self.skip_self = skip_self


class _Tracker:
    def __init__(self):
        self.last_w = {}
        self.readers = {}
        self.dma_counts = {}

    def _deps(self, eng, reads, writes):
        evs = {}

        def add(s, v):
            if evs.get(s, 0) < v:
                evs[s] = v
        for k in reads:
            ev = self.last_w.get(k)
            assert ev is not None, "read of never-written key %r" % (k,)
            add(*ev)
        for k in writes:
            ev = self.last_w.get(k)
            if ev is not None:
                add(*ev)
            for s, v in self.readers.get(k, {}).items():
                add(s, v)
        out = []
        for s, v in evs.items():
            if eng.skip_self and s == eng.semkey:
                continue
            if eng.waited.get(s, 0) >= v:
                continue
            eng.waited[s] = v
            out.append((s, v))
        return out

    def _commit(self, ev, reads, writes):
        for k in writes:
            self.last_w[k] = ev
            self.readers[k] = {}
        for k in reads:
            r = self.readers.setdefault(k, {})
            if r.get(ev[0], 0) < ev[1]:
                r[ev[0]] = ev[1]

    def emit(self, eng, fn, reads=(), writes=()):
        waits = self._deps(eng, reads, writes)
        eng.count += 1
        ev = (eng.semkey, eng.count)
        eng.prog.append((waits, fn, eng.semkey, 1))
        self._commit(ev, reads, writes)
        return ev

    def emit_dma(self, eng, fn, semkey, reads=(), writes=()):
        waits = self._deps(eng, reads, writes)
        c = self.dma_counts.get(semkey, 0) + 16
        self.dma_counts[semkey] = c
        ev = (semkey, c)
        eng.prog.append((waits, fn, semkey, 16))
        self._commit(ev, reads, writes)
        return ev

    def emit_dma_batch(self, eng, fns, semkey, reads=(), writes=()):
        waits = self._deps(eng, reads, writes)
        waits = [(s_, v_) for (s_, v_) in waits if s_ != semkey]
        c = self.dma_counts.get(semkey, 0) + 16 * len(fns)
        self.dma_counts[semkey] = c
        ev = (semkey, c)
        for i, fn in enumerate(fns):
            eng.prog.append((waits if i == 0 else [], fn, semkey, 16))
        self._commit(ev, reads, writes)
        return ev

    def emit_wait_only(self, eng, reads):
        waits = self._deps(eng, reads, ())
        eng.prog.append((waits, None, None, 0))


def build_program(n_layers, nseq, seq, final_norm):
    ntok = nseq * seq
    tiles_per_seq = seq // T
    ntiles = nseq * tiles_per_seq
    L = n_layers

    nc = bass.Bass("TRN2", target_bir_lowering=False)
    xin = nc.dram_tensor("xin", [128, KC, ntok], F32, kind="ExternalInput").ap()
    wall = nc.dram_tensor("wall", [L, 128, NB * 128], F32, kind="ExternalInput").ap()
    pvec = nc.dram_tensor("pvec", [128, NPL * L + 8], F32, kind="ExternalInput").ap()
    wst = nc.dram_tensor("wst", [128, L * 8 * 128], F32, kind="ExternalInput").ap()
    bsr = nc.dram_tensor("bsr", [128, L * 8 * 128], F32, kind="ExternalInput").ap()
    yout = nc.dram_tensor("yout", [128, KC, ntok], F32, kind="ExternalOutput").ap()
    wbf = nc.dram_tensor("wbf", [L, 128, NB * 128], BF16, kind="Internal").ap()
    cdr = nc.dram_tensor("cdr", [4, 128, TW + 2], BF16, kind="Internal").ap()
    xs = [nc.dram_tensor("xs%d" % i, [128, KC, ntok], F32, kind="Internal").ap() for i in range(max(L - 1, 0))]

    tr = _Tracker()
    PE = _Eng("pe", skip_self=True)
    ACT = _Eng("act")
    DVE = _Eng("dve")
    POOL = _Eng("pool")
    SP = _Eng("sp", skip_self=True)
    engines = [PE, ACT, DVE, POOL, SP]

    with ExitStack() as es:
        def sb(name, shape, dt):
            return es.enter_context(nc.sbuf_tensor(name, shape, dt))

        xt = [sb("xt%d" % b, [128, KC, TW], F32) for b in range(2)]
        sq = sb("sq", [128, KC, TW], BF16)
        h = sb("h", [128, KC, TW], BF16)
        h2 = sb("h2", [128, KC, T], BF16)
        rstd = sb("rstd", [128, TW], F32)
        c_sb = sb("c_sb", [128, NCS, TW + 2], BF16)
        cst = sb("cst", [128, NCST, 4, CSW], BF16)
        y_sb = sb("y_sb", [128, KC, T], BF16)
        vz = sb("vz", [128, 4 * D], BF16)
        vhat = vz[:, :].rearrange("p (a b) -> p a b", b=D)
        z_sb = vz[:, :].rearrange("p (a b) -> p a b", b=T)
        a_sb = sb("a_sb", [128, NJ, T], BF16)
        vst = sb("vst", [128, 4, 2, 6], F32)
        vmv = sb("vmv", [128, 4, 2], F32)
        vsc = sb("vsc", [128, 4, 2], F32)
        u_sb = sb("u_sb", [128, KC, T], BF16)
        sgth = sb("sgth", [128, 2, 32], F32)
        siga = sb("siga", [128, KC, T], BF16)
        sigb = sb("sigb", [128, KC, T], BF16)
        st_mean = sb("st_mean", [128, T], F32)
        st_rstd = sb("st_rstd", [128, T], F32)
        st_mr = sb("st_mr", [128, T], F32)
        NTMP = 3
        tmp = sb("tmp", [128, NTMP, T], F32)
        gm = u_sb
        wring = sb("wring", [128, RING, CHB * 128], BF16)
        ones = sb("ones", [128, 128], BF16)
        epst = sb("epst", [128, 1], F32)
        pv = sb("pv", [128, NPL * L + 8], F32)
        wst_f = tmp[:, 0:2, :].rearrange("p a b -> p (a b)")
        bsr_v = xt[1][:, 0:2, 0:T]
        wst_b = sb("wst_b", [128, L * 8 * 128], BF16)
        Bsp = sb("Bsp", [128, L * 8, 128], F32)
        ps = es.enter_context(nc.psum_tensor("ps", [128, 8, 512], F32))

        NRING_PS = 6
        state = {"psb": 0, "tmp": 0, "halo": 0, "wfill": 0}

        def next_bank():
            b = state["psb"]
            state["psb"] = (b + 1) % NRING_PS
            return b

        def next_tmp():
            t = state["tmp"]
            state["tmp"] = (t + 1) % NTMP
            return t

        def next_halo():
            s = state["halo"]
            state["halo"] = (s + 1) % 2
            return 6 + s

        def pcol(li, off, m):
            c = li * NPL + off + m
            return pv[:, c:c + 1]

        tr.emit(DVE, lambda e: e.memset(ones[:], 1.0), writes=["ones"])
        tr.emit(DVE, lambda e: e.memset(epst[:], EPS), writes=["eps"])
        tr.emit_dma(SP, lambda e: e.dma_start(out=pv[:], in_=pvec[:, :]), "sem_pv", writes=["pv"])

        seqn = [(li, ti) for li in range(L) for ti in range(ntiles)]

        def emit_cast(li, c):
            b0 = c * CPB
            b1 = min(NB, b0 + CPB)

            def f(e):
                src = wall[li, :, b0 * 128:b1 * 128].rearrange("p (a b) -> p a b", b=1024)
                dst = wbf[li, :, b0 * 128:b1 * 128].rearrange("p (a b) -> p a b", b=1024)
                return e.dma_start(out=dst, in_=src)
            rd = [("wbf", li, c - CWIN)] if c >= CWIN else []
            if li == 0 and c == 0:
                rd = xkeys(0) + [("xth", 0, 1), "pv"]
            if li > 0:
                rd = rd + [("h", 0)]
            tr.emit_dma(POOL, f, "sem_c%d_%d" % (li, c % CSEM), reads=rd, writes=[("wbf", li, c)])
        cast0 = {"next": 0}

        def cast0_upto(n):
            while cast0["next"] < min(n, NPIECE):
                emit_cast(0, cast0["next"])
                cast0["next"] += 1
        tr.emit(DVE, lambda e: e.memset(c_sb[:, :, TW:TW + 2], 0.0), writes=[("cpad",)])

        for li in range(L):
            tr.emit_dma(SP, lambda e, li=li: e.dma_start(out=wst_f[:], in_=wst[:, li * 1024:(li + 1) * 1024]),
                        "sem_wst", writes=["wst_f"])
            tr.emit_dma(SP, lambda e, li=li: e.dma_start(
                out=bsr_v, in_=bsr[:, li * 1024:(li + 1) * 1024].rearrange("p (a b) -> p a b", b=T)),
                "sem_bsr", writes=["bsr_sb", ("xt", 1, 0), ("xt", 1, 1)])
            tr.emit(DVE, lambda e, li=li: e.tensor_copy(out=wst_b[:, li * 1024:(li + 1) * 1024], in_=wst_f[:]),
                    reads=["wst_f"], writes=[("wst_b", li)])
            for hf in range(2):
                bank = next_bank()

                def fpe(e, li=li, hf=hf, bank=bank):
                    return e.matmul(ps[:, bank, :], ones[:],
                                    wst_b[:, li * 1024 + hf * 512: li * 1024 + (hf + 1) * 512],
                                    start=True, stop=True)
                tr.emit(PE, fpe, reads=["ones", ("wst_b", li)], writes=[("ps", bank)])
                for gg in range(4):
                    g = hf * 4 + gg

                    def fd(e, li=li, g=g, gg=gg, bank=bank):
                        return e.scalar_tensor_tensor(
                            out=Bsp[:, li * 8 + g, :], in0=ps[:, bank, gg * 128:(gg + 1) * 128],
                            scalar=pcol(li, 56, g), in1=bsr_v[:, g // 4, (g % 4) * 128:(g % 4 + 1) * 128],
                            op0=ALU.mult, op1=ALU.add)
                    tr.emit(DVE, fd, reads=[("ps", bank), "pv", "bsr_sb", ("xt", 1, 0), ("xt", 1, 1)],
                            writes=[("Bsp", li, g)])

        def src_dram(li):
            return xin if li == 0 else xs[li - 1]

        def dst_dram(li):
            return yout if li == L - 1 else xs[li]

        def xkeys(b):
            return [("xt", b, kc) for kc in range(KC)]

        def xd_keys(li, j):
            return [("xd", li, j, kc) for kc in range(KC)]

        def load_x(li, ti):
            b = (li * ntiles + ti) % 2
            sidx, tt = divmod(ti, tiles_per_seq)
            t0 = sidx * seq + tt * T
            src = src_dram(li)
            rdh = [k for j in (ti - 1, ti, ti + 1) if 0 <= j < ntiles for k in xd_keys(li, j)] if li > 0 else []
            for kc in range(KC):
                rd = [("xd", li, ti, kc)] if li > 0 else []
                tr.emit_dma(SP, lambda e, kc=kc: e.dma_start(out=xt[b][:, kc, 0:T], in_=src[:, kc, t0:t0 + T]),
                            "sem_xm%d_%d" % (b, kc), reads=rd, writes=[("xt", b, kc)])
            if tt > 0:
                tr.emit_dma(SP, lambda e: e.dma_start(out=xt[b][:, :, T:T + HALO], in_=src[:, :, t0 - HALO:t0]),
                            "sem_xl%d" % b, reads=rdh, writes=[("xth", b, 0)])
            else:
                tr.emit(DVE, lambda e: e.memset(xt[b][:, :, T:T + HALO], 0.0), writes=[("xth", b, 0)])
            if tt < tiles_per_seq - 1:
                tr.emit_dma(SP, lambda e: e.dma_start(out=xt[b][:, :, T + HALO:TW], in_=src[:, :, t0 + T:t0 + T + HALO]),
                            "sem_xr%d" % b, reads=rdh, writes=[("xth", b, 1)])
            else:
                tr.emit(DVE, lambda e: e.memset(xt[b][:, :, T + HALO:TW], 0.0), writes=[("xth", b, 1)])

        def store_x_chunk(li, ti, kc):
            b = (li * ntiles + ti) % 2
            sidx, tt = divmod(ti, tiles_per_seq)
            t0 = sidx * seq + tt * T
            dst = dst_dram(li)
            tr.emit_dma(SP, lambda e: e.dma_start(out=dst[:, kc, t0:t0 + T], in_=xt[b][:, kc, 0:T]),
                        "sem_st%d_%d" % (b, kc), reads=[("xt", b, kc)], writes=[("xd", li + 1, ti, kc)])

        SEGS = {"A1": (0, OFF_MIX), "A3": (OFF_MIX, OFF_FFN),
                "B1": (OFF_FFN, OFF_DOWN), "B2": (OFF_DOWN, NB)}
        NIT = len(seqn) + 1
        chunks = []
        seg_base = {}
        for it in range(NIT):
            for seg in ("A1", "B1", "B2", "A3"):
                tl = it if seg[0] == "A" else it - 1
                if tl < 0 or tl >= len(seqn):
                    continue
                seg_base[(it, seg)] = len(chunks)
                b0, b1 = SEGS[seg]
                for blk in range(b0, b1, CHB):
                    chunks.append((seqn[tl][0], blk, min(CHB, b1 - blk)))
        cur = {"it": 0, "in_v": False}
        state["wnext"] = 0

        def issue_wchunk(G):
            li_, blk0, nb = chunks[G]
            s = G % RING
            rd = [("wbf", li_, cc) for cc in range(blk0 // CPB, (blk0 + nb - 1) // CPB + 1)]
            tr.emit_dma(SP, lambda e: e.dma_start(out=wring[:, s, 0:nb * 128],
                                                  in_=wbf[li_, :, blk0 * 128:(blk0 + nb) * 128]),
                        "sem_w%d" % s, reads=rd, writes=[("wslot", s)])

        def touch(gset):
            gmin = seg_base[(cur["it"], "A1")] if cur["in_v"] else min(gset)
            assert max(gset) < gmin + RING
            limit = min(gmin + RING, len(chunks))
            while state["wnext"] < limit:
                issue_wchunk(state["wnext"])
                state["wnext"] += 1

        def wblk(bidx, n=1):
            for seg, (b0, b1) in SEGS.items():
                if b0 <= bidx < b1:
                    break
            c, o = divmod(bidx - b0, CHB)
            assert o + n <= CHB
            G = seg_base[(cur["it"], seg)] + c
            s = G % RING
            return wring[:, s, o * 128:(o + n) * 128], ("wslot", s), G

        def rms_squares(b, width_halo):
            wd = TW if width_halo else T
            for kc in range(KC):
                rd = [("xt", b, kc)] + ([("xth", b, 0), ("xth", b, 1)] if width_halo else [])
                tr.emit(ACT, lambda e, kc=kc: e.activation(out=sq[:, kc, 0:wd], in_=xt[b][:, kc, 0:wd], func=AF.Square),
                        reads=rd, writes=[("sq", kc)])

        def rms_rstd(width_halo):
            bank = next_bank()

            def fpe(e):
                for kc in range(KC):
                    ins = e.matmul(ps[:, bank, :], ones[:], sq[:, kc, 0:T], start=(kc == 0), stop=(kc == KC - 1))
                return ins
            tr.emit(PE, fpe, reads=["ones"] + [("sq", kc) for kc in range(KC)], writes=[("ps", bank)])
            tr.emit(ACT, lambda e: e.activation(out=rstd[:, 0:T], in_=ps[:, bank, :], func=AF.Sqrt,
                                                bias=epst[:, 0:1], scale=1.0 / D),
                    reads=[("ps", bank), "eps"], writes=["rstd_m"])
            if width_halo:
                hs = next_halo()

                def fpe2(e):
                    for kc in range(KC):
                        ins = e.matmul(ps[:, hs, 0:2 * HALO], ones[:], sq[:, kc, T:TW],
                                       start=(kc == 0), stop=(kc == KC - 1))
                    return ins
                tr.emit(PE, fpe2, reads=["ones"] + [("sq", kc) for kc in range(KC)], writes=[("ps", hs)])
                tr.emit(ACT, lambda e: e.activation(out=rstd[:, T:TW], in_=ps[:, hs, 0:2 * HALO],
                                                    func=AF.Sqrt, bias=epst[:, 0:1], scale=1.0 / D),
                        reads=[("ps", hs), "eps"], writes=["rstd_h"])
                tr.emit(DVE, lambda e: e.reciprocal(out=rstd[:, 0:TW], in_=rstd[:, 0:TW]),
                        reads=["rstd_m", "rstd_h"], writes=["rstd_m", "rstd_h"])
            else:
                tr.emit(DVE, lambda e: e.reciprocal(out=rstd[:, 0:T], in_=rstd[:, 0:T]),
                        reads=["rstd_m"], writes=["rstd_m"])

        def rms_stats(b, width_halo):
            rms_squares(b, width_halo)
            rms_rstd(width_halo)

        def norm_to_h_one(b, li, goff, width_halo, kc, hbuf, hname):
            wd = TW if width_halo else T
            rd = [("xt", b, kc), "rstd_m", "pv"] + ([("xth", b, 0), ("xth", b, 1), "rstd_h"] if width_halo else [])
            tr.emit(DVE, lambda e: e.scalar_tensor_tensor(
                out=hbuf[:, kc, 0:wd], in0=xt[b][:, kc, 0:wd], scalar=pcol(li, goff, kc), in1=rstd[:, 0:wd],
                op0=ALU.mult, op1=ALU.mult), reads=rd, writes=[(hname, kc)])

        def norm_to_h(b, li, goff, width_halo, hbuf, hname):
            for kc in range(KC):
                norm_to_h_one(b, li, goff, width_halo, kc, hbuf, hname)

        def hkeys(hname="h"):
            return [(hname, kc) for kc in range(KC)]

        def mm_group(bank_ap, blocks0, rhs_fn, nk, reads, wkey_out, stride=1):
            aps = []
            wkeys = set()
            chunks = set()
            for k in range(nk):
                ap, key, c = wblk(blocks0 + k * stride)
                aps.append(ap)
                wkeys.add(key)
                chunks.add(c)
            touch(chunks)

            def fpe(e):
                for k in range(nk):
                    ins = e.matmul(bank_ap, aps[k], rhs_fn(k), start=(k == 0), stop=(k == nk - 1))
                return ins
            tr.emit(PE, fpe, reads=list(wkeys) + list(reads), writes=[wkey_out])

        def stage_proj(b, li):
            def u_group(m):
                bank = next_bank()
                mm_group(ps[:, bank, :], OFF_U + m * 8, lambda k: h[:, k, 0:T], KC, hkeys(), ("ps", bank))
                tr.emit(ACT, lambda e: e.activation(out=u_sb[:, m, :], in_=ps[:, bank, :], func=AF.Copy),
                        reads=[("ps", bank)], writes=[("u", m)])

            cur["in_v"] = True
            for tc in range(4):
                banks = []
                for hf in range(2):
                    bank = next_bank()
                    banks.append(bank)
                    aps = []
                    wkeys = set()
                    chunks = set()
                    for k in range(KC):
                        ap, key, c = wblk(OFF_V + hf * 32 + k * 4, 4)
                        aps.append(ap)
                        wkeys.add(key)
                        chunks.add(c)
                    touch(chunks)

                    def fpe(e, tc=tc, bank=bank, aps=aps):
                        for k in range(KC):
                            ins = e.matmul(ps[:, bank, :], h[:, k, tc * 128:(tc + 1) * 128], aps[k],
                                           start=(k == 0), stop=(k == KC - 1))
                        return ins
                    tr.emit(PE, fpe, reads=list(wkeys) + hkeys(), writes=[("ps", bank)])
                    tr.emit(DVE, lambda e, tc=tc, hf=hf, bank=bank: e.bn_stats(out=vst[:, tc, hf, :], in_=ps[:, bank, :]),
                            reads=[("ps", bank)], writes=[("vst", tc, hf)])
                tr.emit(DVE, lambda e, tc=tc: e.bn_aggr(out=vmv[:, tc, :], in_=vst[:, tc, :, :].rearrange("p a b -> p (a b)")),
                        reads=[("vst", tc, 0), ("vst", tc, 1)], writes=[("vmv", tc)])
                tr.emit(ACT, lambda e, tc=tc: e.activation(out=vsc[:, tc, 0:1], in_=vmv[:, tc, 1:2], func=AF.Sqrt,
                                                           bias=epst[:, 0:1], scale=1.0),
                        reads=[("vmv", tc), "eps"], writes=[("vsc0", tc)])
                tr.emit(DVE, lambda e, tc=tc: e.reciprocal(out=vsc[:, tc, 0:1], in_=vsc[:, tc, 0:1]),
                        reads=[("vsc0", tc)], writes=[("vsc0", tc)])
                tr.emit(DVE, lambda e, tc=tc: e.scalar_tensor_tensor(
                    out=vsc[:, tc, 1:2], in0=vmv[:, tc, 0:1], scalar=-1.0, in1=vsc[:, tc, 0:1],
                    op0=ALU.mult, op1=ALU.mult), reads=[("vmv", tc), ("vsc0", tc)], writes=[("vsc1", tc)])
                for hf in range(2):
                    tr.emit(ACT, lambda e, tc=tc, hf=hf, bank=banks[hf]: e.activation(
                        out=vhat[:, tc, hf * 512:(hf + 1) * 512], in_=ps[:, bank, :], func=AF.Identity,
                        bias=vsc[:, tc, 1:2], scale=vsc[:, tc, 0:1]),
                        reads=[("ps", banks[hf]), ("vsc0", tc), ("vsc1", tc)], writes=[("vhat", tc, hf)])
                u_group(2 * tc)
                u_group(2 * tc + 1)
            cur["in_v"] = False
            def gv_pos(m):
                return OFF_GV + 16 * m + 8 * max(0, m - 3)

            def cu_pos(m):
                return gv_pos(m + 3) + 16 if m <= 4 else OFF_CU5 + (m - 5) * 8

            def conv_chunk(m):
                slot = m % NCST
                bank = next_bank()
                ap, key, G = wblk(cu_pos(m), 8)
                touch({G})

                def fpe(e):
                    for q in range(8):
                        for j in range(4):
                            ins = e.matmul(ps[32 * j:32 * j + 32, bank, :], ap[:, (q * 4 + j) * 32:(q * 4 + j + 1) * 32],
                                           cst[:, slot, j, 4 * q:4 * q + T], start=(q == 0), stop=(q == 7),
                                           tile_position=(0, 32 * j))
                    return ins
                tr.emit(PE, fpe, reads=[key, ("cst", slot)], writes=[("ps", bank)])
                tr.emit(ACT, lambda e: e.activation(out=y_sb[:, m, :], in_=ps[:, bank, :], func=AF.Identity,
                                                    bias=pcol(li, 24, m), scale=1.0),
                        reads=[("ps", bank), "pv"], writes=[("y", m)])
                tr.emit(ACT, lambda e: e.activation(out=sq[:, m, 0:T], in_=ps[:, bank, :], func=AF.Square,
                                                    bias=pcol(li, 24, m), scale=1.0),
                        reads=[("ps", bank), "pv"], writes=[("sq", m)])

            for m in range(KC):
                sl = m % 2
                cs_ = m % NCS
                bank_g = next_bank()
                hs_g = next_halo()
                mm_group(ps[:, bank_g, :], gv_pos(m), lambda k: h[:, k, 0:T], KC, hkeys(), ("ps", bank_g))
                mm_group(ps[:, hs_g, 0:2 * HALO], gv_pos(m), lambda k: h[:, k, T:TW], KC,
                         hkeys(), ("ps", hs_g))
                ts_ = next_tmp()
                tk = [("tmp", ts_, tc) for tc in range(4)]
                tr.emit(ACT, lambda e, ts_=ts_, bank=bank_g: e.activation(out=tmp[:, ts_, :], in_=ps[:, bank, :],
                                                                           func=AF.Sigmoid),
                        reads=[("ps", bank_g)], writes=tk)
                tr.emit(ACT, lambda e, sl=sl, hs=hs_g: e.activation(out=sgth[:, sl, 0:2 * HALO], in_=ps[:, hs, 0:2 * HALO],
                                                                     func=AF.Sigmoid),
                        reads=[("ps", hs_g)], writes=[("sgth", sl)])
                bank_v = next_bank()
                hs_v = next_halo()
                mm_group(ps[:, bank_v, :], gv_pos(m) + 8, lambda k: h[:, k, 0:T], KC, hkeys(), ("ps", bank_v))
                mm_group(ps[:, hs_v, 0:2 * HALO], gv_pos(m) + 8, lambda k: h[:, k, T:TW], KC,
                         hkeys(), ("ps", hs_v))
                tr.emit(DVE, lambda e, cs_=cs_, ts_=ts_, bank=bank_v: e.tensor_tensor(
                    out=c_sb[:, cs_, HALO:HALO + T], in0=ps[:, bank, :], in1=tmp[:, ts_, :], op=ALU.mult),
                    reads=[("ps", bank_v)] + tk, writes=[("c", cs_)])
                tr.emit(DVE, lambda e, cs_=cs_, sl=sl, hs=hs_v: e.tensor_tensor(
                    out=c_sb[:, cs_, 0:HALO], in0=ps[:, hs, 0:HALO], in1=sgth[:, sl, 0:HALO], op=ALU.mult),
                    reads=[("ps", hs_v), ("sgth", sl)], writes=[("ch0", cs_)])
                tr.emit(DVE, lambda e, cs_=cs_, sl=sl, hs=hs_v: e.tensor_tensor(
                    out=c_sb[:, cs_, HALO + T:TW], in0=ps[:, hs, HALO:2 * HALO],
                    in1=sgth[:, sl, HALO:2 * HALO], op=ALU.mult),
                    reads=[("ps", hs_v), ("sgth", sl)], writes=[("ch1", cs_)])
                slot = m % NCST
                fns = []
                ds = m % 4
                tr.emit_dma(SP, lambda e, cs_=cs_, ds=ds: e.dma_start(out=cdr[ds, :, :], in_=c_sb[:, cs_, :]),
                            "sem_cdw%d" % ds, reads=[("c", cs_), ("ch0", cs_), ("ch1", cs_), ("cpad",)],
                            writes=[("cdr", ds)])
                for s_ in range(4):
                    fns.append(lambda e, s_=s_, ds=ds, slot=slot: e.dma_start(
                        out=cst[32 * s_:32 * s_ + 32, slot, :, :],
                        in_=cdr[ds, :, s_:s_ + CSW].rearrange("(j c) x -> c j x", j=4)))
                tr.emit_dma_batch(SP, fns, "sem_cst%d" % slot, reads=[("cdr", ds)], writes=[("cst", slot)])
                if m >= 3:
                    conv_chunk(m - 3)
            for m in range(KC):
                for which, dstt, boff in ((0, siga, 8), (1, sigb, 16)):
                    bank = next_bank()
                    mm_group(ps[:, bank, :], OFF_GAB + m * 16 + which * 8, lambda k: h[:, k, 0:T], KC, hkeys(), ("ps", bank))
                    tr.emit(ACT, lambda e, m=m, bank=bank, dstt=dstt, boff=boff: e.activation(
                        out=dstt[:, m, :], in_=ps[:, bank, :], func=AF.Sigmoid, bias=pcol(li, boff, m), scale=1.0),
                        reads=[("ps", bank), "pv"], writes=[("sig", which, m)])
            for m in (5, 6, 7):
                conv_chunk(m)

        def stage_spatial(li):
            for g in range(KC):
                bank = next_bank()

                def fpe(e, g=g, bank=bank):
                    for tc in range(4):
                        ins = e.matmul(ps[:, bank, tc * 128:(tc + 1) * 128], vhat[:, tc, g * 128:(g + 1) * 128],
                                       wst_b[:, li * 1024 + g * 128: li * 1024 + (g + 1) * 128], start=True, stop=True)
                    return ins
                tr.emit(PE, fpe, reads=[("vhat", tc, g // 4) for tc in range(4)] + [("wst_b", li)], writes=[("ps", bank)])
                tslot = next_tmp()
                for tc in range(4):
                    tr.emit(DVE, lambda e, g=g, tc=tc, bank=bank, tslot=tslot: e.scalar_tensor_tensor(
                        out=tmp[:, tslot, tc * 128:(tc + 1) * 128], in0=ps[:, bank, tc * 128:(tc + 1) * 128],
                        scalar=pcol(li, 48, g), in1=Bsp[:, li * 8 + g, :], op0=ALU.mult, op1=ALU.add),
                        reads=[("ps", bank), "pv", ("Bsp", li, g)], writes=[("tmp", tslot, tc)])
                tr.emit(DVE, lambda e, g=g, tslot=tslot: e.tensor_tensor(out=gm[:, g, :], in0=tmp[:, tslot, :],
                                                                           in1=u_sb[:, g, :], op=ALU.mult),
                        reads=[("tmp", tslot, tc) for tc in range(4)] + [("u", g)], writes=[("u", g)])

        def stage_conv_ln(li):
            b1 = next_bank()
            b2 = next_bank()

            def fpe1(e):
                for kc in range(KC):
                    ins = e.matmul(ps[:, b1, :], ones[:], y_sb[:, kc, :], start=(kc == 0), stop=(kc == KC - 1))
                return ins

            def fpe2(e):
                for kc in range(KC):
                    ins = e.matmul(ps[:, b2, :], ones[:], sq[:, kc, 0:T], start=(kc == 0), stop=(kc == KC - 1))
                return ins
            tr.emit(PE, fpe1, reads=["ones"] + [("y", m) for m in range(KC)], writes=[("ps", b1)])
            tr.emit(PE, fpe2, reads=["ones"] + [("sq", m) for m in range(KC)], writes=[("ps", b2)])
            tr.emit(DVE, lambda e: e.tensor_scalar(out=st_mean[:], in0=ps[:, b1, :], scalar1=1.0 / D, scalar2=None,
                                                   op0=ALU.mult), reads=[("ps", b1)], writes=["st_mean"])
            tr.emit(DVE, lambda e: e.tensor_tensor(out=st_mr[:], in0=st_mean[:], in1=st_mean[:], op=ALU.mult),
                    reads=["st_mean"], writes=["st_mr"])
            tr.emit(DVE, lambda e: e.scalar_tensor_tensor(out=st_rstd[:], in0=ps[:, b2, :], scalar=1.0 / D, in1=st_mr[:],
                                                          op0=ALU.mult, op1=ALU.subtract),
                    reads=[("ps", b2), "st_mr"], writes=["st_rstd"])
            tr.emit(ACT, lambda e: e.activation(out=st_rstd[:], in_=st_rstd[:], func=AF.Sqrt, bias=epst[:, 0:1], scale=1.0),
                    reads=["st_rstd", "eps"], writes=["st_rstd"])
            tr.emit(DVE, lambda e: e.reciprocal(out=st_rstd[:], in_=st_rstd[:]), reads=["st_rstd"], writes=["st_rstd"])
            tr.emit(DVE, lambda e: e.tensor_tensor(out=st_mr[:], in0=st_mean[:], in1=st_rstd[:], op=ALU.mult),
                    reads=["st_mean", "st_rstd"], writes=["st_mr"])
            for m in range(KC):
                tslot = next_tmp()
                tk = [("tmp", tslot, tc) for tc in range(4)]
                tr.emit(DVE, lambda e, m=m, tslot=tslot: e.tensor_tensor(out=tmp[:, tslot, :], in0=y_sb[:, m, :],
                                                                           in1=st_rstd[:], op=ALU.mult),
                        reads=[("y", m), "st_rstd"], writes=tk)
                tr.emit(DVE, lambda e, tslot=tslot: e.tensor_tensor(out=tmp[:, tslot, :], in0=tmp[:, tslot, :],
                                                                      in1=st_mr[:], op=ALU.subtract),
                        reads=tk + ["st_mr"], writes=tk)
                tr.emit(ACT, lambda e, m=m, tslot=tslot: e.activation(out=z_sb[:, m, :], in_=tmp[:, tslot, :], func=AF.Silu,
                                                                       bias=pcol(li, 40, m), scale=pcol(li, 32, m)),
                        reads=tk + ["pv"], writes=[("z", m)])

        def stage_mix_pairs(li):
            gk = [("u", k) for k in range(KC)]
            zk = [("z", k) for k in range(KC)]
            for m in range(KC):
                bank_b = next_bank()
                mm_group(ps[:, bank_b, :], OFF_MIX + m * 16, lambda k: gm[:, k, :], KC, gk, ("ps", bank_b))
                bank_a = next_bank()
                mm_group(ps[:, bank_a, :], OFF_MIX + m * 16 + 8, lambda k: z_sb[:, k, :], KC, zk, ("ps", bank_a))
                t1 = next_tmp()
                t2 = next_tmp()
                k1 = [("tmp", t1, tc) for tc in range(4)]
                k2 = [("tmp", t2, tc) for tc in range(4)]
                tr.emit(DVE, lambda e, m=m, bank=bank_b, t1=t1: e.tensor_tensor(out=tmp[:, t1, :], in0=ps[:, bank, :],
                                                                                 in1=sigb[:, m, :], op=ALU.mult),
                        reads=[("ps", bank_b), ("sig", 1, m)], writes=k1)
                tr.emit(DVE, lambda e, m=m, bank=bank_a, t2=t2: e.tensor_tensor(out=tmp[:, t2, :], in0=ps[:, bank, :],
                                                                                 in1=siga[:, m, :], op=ALU.mult),
                        reads=[("ps", bank_a), ("sig", 0, m)], writes=k2)
                tr.emit(DVE, lambda e, m=m, t1=t1, t2=t2: e.tensor_tensor(out=a_sb[:, m, :], in0=tmp[:, t1, :],
                                                                            in1=tmp[:, t2, :], op=ALU.add),
                        reads=k1 + k2, writes=[("a", m)])

        def stage_wo(b, li, after_each=None):
            mk = [("a", k) for k in range(KC)]
            for m in range(KC):
                bank = next_bank()
                mm_group(ps[:, bank, :], OFF_WO + m * 8, lambda k: a_sb[:, k, :], KC, mk, ("ps", bank))
                tr.emit(DVE, lambda e, m=m, bank=bank: e.tensor_tensor(out=xt[b][:, m, 0:T], in0=ps[:, bank, :],
                                                                        in1=xt[b][:, m, 0:T], op=ALU.add),
                        reads=[("ps", bank), ("xt", b, m)], writes=[("xt", b, m)])
                if after_each is not None:
                    after_each(m)

        def stage_ffn_up(li):
            for j in range(NJ):
                bank_g = next_bank()
                mm_group(ps[:, bank_g, :], OFF_FFN + j * 16, lambda k: h2[:, k, :], KC, hkeys("h2"), ("ps", bank_g))
                bank_u = next_bank()
                mm_group(ps[:, bank_u, :], OFF_FFN + j * 16 + 8, lambda k: h2[:, k, :], KC, hkeys("h2"), ("ps", bank_u))
                tslot = next_tmp()
                tk = [("tmp", tslot, tc) for tc in range(4)]
                tr.emit(ACT, lambda e, bank=bank_g, tslot=tslot: e.activation(out=tmp[:, tslot, :], in_=ps[:, bank, :],
                                                                               func=AF.Silu),
                        reads=[("ps", bank_g)], writes=tk)
                tr.emit(DVE, lambda e, j=j, bank=bank_u, tslot=tslot: e.tensor_tensor(
                    out=a_sb[:, j, :], in0=ps[:, bank, :], in1=tmp[:, tslot, :], op=ALU.mult),
                    reads=[("ps", bank_u)] + tk, writes=[("a", j)])

        def stage_ffn_down(b, li, after_each=None):
            ak = [("a", j) for j in range(NJ)]
            for m in range(KC):
                bank = next_bank()
                mm_group(ps[:, bank, :], OFF_DOWN + m * NJ, lambda k: a_sb[:, k, :], NJ, ak, ("ps", bank))
                tr.emit(DVE, lambda e, m=m, bank=bank: e.tensor_tensor(out=xt[b][:, m, 0:T], in0=ps[:, bank, :],
                                                                        in1=xt[b][:, m, 0:T], op=ALU.add),
                        reads=[("ps", bank), ("xt", b, m)], writes=[("xt", b, m)])
                if after_each is not None:
                    after_each(m)

        def stage_final_norm(b, after_each=None):
            rms_stats(b, False)
            for kc in range(KC):
                c = L * NPL + kc
                tr.emit(DVE, lambda e, kc=kc, c=c: e.scalar_tensor_tensor(
                    out=xt[b][:, kc, 0:T], in0=xt[b][:, kc, 0:T], scalar=pv[:, c:c + 1], in1=rstd[:, 0:T],
                    op0=ALU.mult, op1=ALU.mult), reads=[("xt", b, kc), "rstd_m", "pv"], writes=[("xt", b, kc)])
                if after_each is not None:
                    after_each(kc)

        NTL = len(seqn)

        def emit_casts_for(n):
            li, ti = seqn[n]
            if li + 1 < L:
                per = (NPIECE + ntiles - 1) // ntiles
                for c in range(ti * per, min((ti + 1) * per, NPIECE)):
                    emit_cast(li + 1, c)

        load_x(*seqn[0])
        cast0_upto(NPIECE)
        rms_stats(0, True)
        norm_to_h(0, seqn[0][0], 0, True, h, "h")
        for it in range(NTL + 1):
            cur["it"] = it
            hasA = it < NTL
            hasB = it >= 1
            if hasA:
                li, ti = seqn[it]
                b = it % 2
                emit_casts_for(it)
            if hasB:
                lb, tb = seqn[it - 1]
                bb = (it - 1) % 2
            if hasA:
                stage_proj(b, li)
                if it == 0:
                    cast0_upto(8)
            if hasB:
                stage_ffn_up(lb)
            if hasA:
                stage_spatial(li)
                stage_conv_ln(li)
            if hasB:
                st_hook = lambda kc, lb=lb, tb=tb: store_x_chunk(lb, tb, kc)
                if final_norm and lb == L - 1:
                    stage_ffn_down(bb, lb)
                    stage_final_norm(bb, st_hook)
                else:
                    stage_ffn_down(bb, lb, st_hook)
            if it + 1 < NTL:
                load_x(*seqn[it + 1])
                rms_squares((it + 1) % 2, True)
            if hasA:
                stage_mix_pairs(li)
                if it == 0:
                    cast0_upto(NPIECE)
            if it + 1 < NTL:
                rms_rstd(True)
            if hasA:
                nxt = (it + 1) % 2
                lnx = seqn[it + 1][0] if it + 1 < NTL else None
                hook = (lambda m: norm_to_h_one(nxt, lnx, 0, True, m, h, "h")) if it + 1 < NTL else None
                stage_wo(b, li, hook)
                rms_stats(b, False)
                norm_to_h(b, li, 64, False, h2, "h2")
        tr.emit_wait_only(SP, [k for ti in range(ntiles) for k in xd_keys(L, ti)])

        semkeys = set()
        for eng in engines:
            for waits, fn, sk, inc in eng.prog:
                for s, v in waits:
                    semkeys.add(s)
                if sk:
                    semkeys.add(sk)
        sems = {k: es.enter_context(nc.semaphore(k)) for k in sorted(semkeys)}

        def replay(eng, handle):
            for waits, fn, sk, inc in eng.prog:
                for s, v in waits:
                    handle.wait_ge(sems[s], v)
                if fn is not None:
                    ins = fn(handle)
                    ins.then_inc(sems[sk], inc)

        block = es.enter_context(nc.Block())
        block.tensor(lambda e: replay(PE, e))
        block.scalar(lambda e: replay(ACT, e))
        block.vector(lambda e: replay(DVE, e))
        block.gpsimd(lambda e: replay(POOL, e))
        block.sync(lambda e: replay(SP, e))
    return nc


def _prep_params(inputs, layers, with_final):
    L = len(layers)
    wall = np.stack([_layer_blocks(inputs["w_in"][l], inputs["conv_w"][l], inputs["w_conv_out"][l],
                                   inputs["w_sgu_out"][l], inputs["w_o"][l], inputs["w_ffn_gate"][l],
                                   inputs["w_ffn_up"][l], inputs["w_ffn_down"][l]) for l in layers])
    pvec = np.zeros((128, NPL * L + 8), np.float32)
    wst = np.zeros((128, L * 1024), np.float32)
    bsr = np.zeros((128, L * 1024), np.float32)
    for i, l in enumerate(layers):
        o = i * NPL
        pvec[:, o + 0:o + 8] = _vec8(inputs["norm_mix"][l])
        pvec[:, o + 8:o + 16] = _vec8(inputs["gate_bias"][l][:D])
        pvec[:, o + 16:o + 24] = _vec8(inputs["gate_bias"][l][D:])
        pvec[:, o + 24:o + 32] = _vec8(inputs["conv_b"][l])
        pvec[:, o + 32:o + 40] = _vec8(inputs["conv_ln_g"][l])
        pvec[:, o + 40:o + 48] = _vec8(inputs["conv_ln_b"][l])
        pvec[:, o + 48:o + 56] = _vec8(inputs["sgu_ln_g"][l])
        pvec[:, o + 56:o + 64] = _vec8(inputs["sgu_ln_b"][l])
        pvec[:, o + 64:o + 72] = _vec8(inputs["norm_ffn"][l])
        wst[:, i * 1024:(i + 1) * 1024] = inputs["w_spatial"][l].transpose(2, 0, 1).reshape(128, 1024)
        bsr[:, i * 1024:(i + 1) * 1024] = np.broadcast_to(inputs["b_spatial"][l].reshape(1, 1024), (128, 1024))
    if with_final:
        pvec[:, NPL * L:NPL * L + 8] = _vec8(inputs["norm_final"])
    return wall, pvec, wst, bsr


def _to_fm(x, ncores):
    B, S, _ = x.shape
    nseq = B // ncores
    xr = x.reshape(ncores, nseq * S, 8, 128)
    return np.ascontiguousarray(xr.transpose(0, 3, 2, 1))


def _from_fm(y, B, S):
    ncores = y.shape[0]
    return np.ascontiguousarray(y.transpose(0, 3, 2, 1)).reshape(B, S, D)


_PROG_CACHE = {}


def _get_prog(n_layers, nseq, seq, final_norm):
    key = (n_layers, nseq, seq, final_norm)
    if key not in _PROG_CACHE:
        _PROG_CACHE[key] = build_program(n_layers, nseq, seq, final_norm)
    return _PROG_CACHE[key]


def run_layers(xfm, inputs, layers, with_final, ncores, nseq, seq):
    wall, pvec, wst, bsr = _prep_params(inputs, layers, with_final)
    nc = _get_prog(len(layers), nseq, seq, with_final)
    in_maps = [{"xin": xfm[c], "wall": wall, "pvec": pvec, "wst": wst, "bsr": bsr} for c in range(ncores)]
    res = run_bass_kernel_spmd(nc, in_maps, core_ids=list(range(ncores)))
    return np.stack([np.asarray(r["yout"]) for r in res.results])


FUSED = True


def kernel(**inputs):
    inputs = {k: np.asarray(v, np.float32) for k, v in inputs.items()}
    x = inputs["x"]
    B, S, _ = x.shape
    depth = inputs["w_in"].shape[0]
    nseq = B // NCORES
    xfm = _to_fm(x, NCORES)
    if FUSED:
        y = run_layers(xfm, inputs, list(range(depth)), True, NCORES, nseq, S)
    else:
        y = xfm
        for l in range(depth):
            y = run_layers(y, inputs, [l], l == depth - 1, NCORES, nseq, S)
    return _from_fm(y, B, S).astype(np.float32)
```

```python
import numpy as np
from contextlib import ExitStack
import concourse.bass as bass
import concourse.mybir as mybir
from concourse.bass_utils import run_bass_kernel_spmd

F32 = mybir.dt.float32
BF16 = mybir.dt.bfloat16
AF = mybir.ActivationFunctionType
ALU = mybir.AluOpType

D = 1024
KC = 8
DFF = 2816
NJ = 22
CW = 31
HALO = 15
T = 512
TW = T + 2 * HALO
EPS = 1e-6
NCORES = 8

OFF_V = 0
OFF_U = 64
OFF_GV = 128
GVLEN = 128 + 40
OFF_GAB = OFF_GV + GVLEN
OFF_CU5 = OFF_GAB + 128
OFF_MIX = OFF_CU5 + 24
OFF_WO = OFF_MIX + 128
OFF_FFN = OFF_WO + 64
OFF_DOWN = OFF_FFN + 352
NB = OFF_DOWN + 176
CHB = 32
RING = 5
CPB = 16
CWIN = 3
CSEM = 6
NPIECE = (NB + CPB - 1) // CPB
NCS = 2
NCST = 4
CSW = 540
NPL = 72


def _blk(W):
    K, M = W.shape
    return W.reshape(K // 128, 128, M // 128, 128).transpose(1, 0, 2, 3)


def _layer_blocks(w_in, conv_w, w_conv_out, w_sgu_out, w_o, w_gate, w_up, w_down):
    out = np.zeros((128, NB, 128), np.float32)
    bv = _blk(w_in[:, 3072:4096])
    out[:, OFF_V:OFF_V + 64] = bv.reshape(128, 8, 2, 4, 128).transpose(0, 2, 1, 3, 4).reshape(128, 64, 128)
    out[:, OFF_U:OFF_U + 64] = _blk(w_in[:, 2048:3072]).transpose(0, 2, 1, 3).reshape(128, 64, 128)
    bg = _blk(w_in[:, 1024:2048]).transpose(0, 2, 1, 3)
    bva = _blk(w_in[:, 0:1024]).transpose(0, 2, 1, 3)
    wpad = np.zeros((32, 1024), np.float32)
    wpad[:CW] = conv_w.reshape(CW, 1024)
    w5 = wpad.reshape(8, 4, 8, 4, 32)
    cu = np.zeros((8, 4, 32, 8, 4, 32), np.float32)
    ii = np.arange(32)
    cu[:, :, ii, :, :, ii] = w5.transpose(4, 2, 1, 0, 3)
    cu = cu.reshape(8, 128, 8, 128).transpose(1, 0, 2, 3)
    cu = cu.reshape(128, 8, 1024).reshape(128, 8, 8, 128)
    pos = OFF_GV
    for m in range(8):
        out[:, pos:pos + 8] = bg[:, m]
        out[:, pos + 8:pos + 16] = bva[:, m]
        pos += 16
        if m >= 3:
            out[:, pos:pos + 8] = cu[:, m - 3]
            pos += 8
    assert pos == OFF_GAB
    for m in (5, 6, 7):
        out[:, OFF_CU5 + (m - 5) * 8:OFF_CU5 + (m - 4) * 8] = cu[:, m]
    ba = _blk(w_in[:, 4096:5120]).transpose(0, 2, 1, 3)
    bb = _blk(w_in[:, 5120:6144]).transpose(0, 2, 1, 3)
    out[:, OFF_GAB:OFF_GAB + 128] = np.stack([ba, bb], axis=2).reshape(128, 128, 128)
    bs = _blk(w_sgu_out).transpose(0, 2, 1, 3)
    bc = _blk(w_conv_out).transpose(0, 2, 1, 3)
    out[:, OFF_MIX:OFF_MIX + 128] = np.stack([bs, bc], axis=2).reshape(128, 128, 128)
    out[:, OFF_WO:OFF_WO + 64] = _blk(w_o).transpose(0, 2, 1, 3).reshape(128, 64, 128)
    fg = _blk(w_gate).transpose(0, 2, 1, 3)
    fu = _blk(w_up).transpose(0, 2, 1, 3)
    out[:, OFF_FFN:OFF_FFN + 352] = np.stack([fg, fu], axis=2).reshape(128, 352, 128)
    out[:, OFF_DOWN:OFF_DOWN + 176] = _blk(w_down).transpose(0, 2, 1, 3).reshape(128, 176, 128)
    return out.reshape(128, NB * 128)


def _vec8(v):
    return np.ascontiguousarray(np.asarray(v, np.float32).reshape(8, 128).T)


class _Eng:
    def __init__(self, name, skip_self=False):
        self.name = name
        self.semkey = "prog_" + name
        self.count = 0
        self.waited = {}
        self.prog = []
        self.skip_self = skip_self


class _Tracker:
    def __init__(self):
        self.last_w = {}
        self.readers = {}
        self.dma_counts = {}

    def _deps(self, eng, reads, writes):
        evs = {}

        def add(s, v):
            if evs.get(s, 0) < v:
                evs[s] = v
        for k in reads:
            ev = self.last_w.get(k)
            assert ev is not None, "read of never-written key %r" % (k,)
            add(*ev)
        for k in writes:
            ev = self.last_w.get(k)
            if ev is not None:
                add(*ev)
            for s, v in self.readers.get(k, {}).items():
                add(s, v)
        out = []
        for s, v in evs.items():
            if eng.skip_self and s == eng.semkey:
                continue
            if eng.waited.get(s, 0) >= v:
                continue
            eng.waited[s] = v
            out.append((s, v))
        return out

    def _commit(self, ev, reads, writes):
        for k in writes:
            self.last_w[k] = ev
            self.readers[k] = {}
        for k in reads:
            r = self.readers.setdefault(k, {})
            if r.get(ev[0], 0) < ev[1]:
                r[ev[0]] = ev[1]

    def emit(self, eng, fn, reads=(), writes=()):
        waits = self._deps(eng, reads, writes)
        eng.count += 1
        ev = (eng.semkey, eng.count)
        eng.prog.append((waits, fn, eng.semkey, 1))
        self._commit(ev, reads, writes)
        return ev

    def emit_dma(self, eng, fn, semkey, reads=(), writes=()):
        waits = self._deps(eng, reads, writes)
        c = self.dma_counts.get(semkey, 0) + 16
        self.dma_counts[semkey] = c
        ev = (semkey, c)
        eng.prog.append((waits, fn, semkey, 16))
        self._commit(ev, reads, writes)
        return ev

    def emit_dma_batch(self, eng, fns, semkey, reads=(), writes=()):
        waits = self._deps(eng, reads, writes)
        waits = [(s_, v_) for (s_, v_) in waits if s_ != semkey]
        c = self.dma_counts.get(semkey, 0) + 16 * len(fns)
        self.dma_counts[semkey] = c
        ev = (semkey, c)
        for i, fn in enumerate(fns):
            eng.prog.append((waits if i == 0 else [], fn, semkey, 16))
        self._commit(ev, reads, writes)
        return ev

    def emit_wait_only(self, eng, reads):
        waits = self._deps(eng, reads, ())
        eng.prog.append((waits, None, None, 0))


def build_program(n_layers, nseq, seq, final_norm):
    ntok = nseq * seq
    tiles_per_seq = seq // T
    ntiles = nseq * tiles_per_seq
    L = n_layers

    nc = bass.Bass("TRN2", target_bir_lowering=False)
    xin = nc.dram_tensor("xin", [128, KC, ntok], F32, kind="ExternalInput").ap()
    wall = nc.dram_tensor("wall", [L, 128, NB * 128], F32, kind="ExternalInput").ap()
    pvec = nc.dram_tensor("pvec", [128, NPL * L + 8], F32, kind="ExternalInput").ap()
    wst = nc.dram_tensor("wst", [128, L * 8 * 128], F32, kind="ExternalInput").ap()
    bsr = nc.dram_tensor("bsr", [128, L * 8 * 128], F32, kind="ExternalInput").ap()
    yout = nc.dram_tensor("yout", [128, KC, ntok], F32, kind="ExternalOutput").ap()
    wbf = nc.dram_tensor("wbf", [L, 128, NB * 128], BF16, kind="Internal").ap()
    cdr = nc.dram_tensor("cdr", [4, 128, TW + 2], BF16, kind="Internal").ap()
    xs = [nc.dram_tensor("xs%d" % i, [128, KC, ntok], F32, kind="Internal").ap() for i in range(max(L - 1, 0))]

    tr = _Tracker()
    PE = _Eng("pe", skip_self=True)
    ACT = _Eng("act")
    DVE = _Eng("dve")
    POOL = _Eng("pool")
    SP = _Eng("sp", skip_self=True)
    engines = [PE, ACT, DVE, POOL, SP]

    with ExitStack() as es:
        def sb(name, shape, dt):
            return es.enter_context(nc.sbuf_tensor(name, shape, dt))

        xt = [sb("xt%d" % b, [128, KC, TW], F32) for b in range(2)]
        sq = sb("sq", [128, KC, TW], BF16)
        h = sb("h", [128, KC, TW], BF16)
        h2 = sb("h2", [128, KC, T], BF16)
        rstd = sb("rstd", [128, TW], F32)
        c_sb = sb("c_sb", [128, NCS, TW + 2], BF16)
        cst = sb("cst", [128, NCST, 4, CSW], BF16)
        y_sb = sb("y_sb", [128, KC, T], BF16)
        vz = sb("vz", [128, 4 * D], BF16)
        vhat = vz[:, :].rearrange("p (a b) -> p a b", b=D)
        z_sb = vz[:, :].rearrange("p (a b) -> p a b", b=T)
        a_sb = sb("a_sb", [128, NJ, T], BF16)
        vst = sb("vst", [128, 4, 2, 6], F32)
        vmv = sb("vmv", [128, 4, 2], F32)
        vsc = sb("vsc", [128, 4, 2], F32)
        u_sb = sb("u_sb", [128, KC, T], BF16)
        sgth = sb("sgth", [128, 2, 32], F32)
        siga = sb("siga", [128, KC, T], BF16)
        sigb = sb("sigb", [128, KC, T], BF16)
        st_mean = sb("st_mean", [128, T], F32)
        st_rstd = sb("st_rstd", [128, T], F32)
        st_mr = sb("st_mr", [128, T], F32)
        NTMP = 3
        tmp = sb("tmp", [128, NTMP, T], F32)
        gm = u_sb
        wring = sb("wring", [128, RING, CHB * 128], BF16)
        ones = sb("ones", [128, 128], BF16)
        epst = sb("epst", [128, 1], F32)
        pv = sb("pv", [128, NPL * L + 8], F32)
        wst_f = tmp[:, 0:2, :].rearrange("p a b -> p (a b)")
        bsr_v = xt[1][:, 0:2, 0:T]
        wst_b = sb("wst_b", [128, L * 8 * 128], BF16)
        Bsp = sb("Bsp", [128, L * 8, 128], F32)
        ps = es.enter_context(nc.psum_tensor("ps", [128, 8, 512], F32))

        NRING_PS = 6
        state = {"psb": 0, "tmp": 0, "halo": 0, "wfill": 0}

        def next_bank():
            b = state["psb"]
            state["psb"] = (b + 1) % NRING_PS
            return b

        def next_tmp():
            t = state["tmp"]
            state["tmp"] = (t + 1) % NTMP
            return t

        def next_halo():
            s = state["halo"]
            state["halo"] = (s + 1) % 2
            return 6 + s

        def pcol(li, off, m):
            c = li * NPL + off + m
            return pv[:, c:c + 1]

        tr.emit(DVE, lambda e: e.memset(ones[:], 1.0), writes=["ones"])
        tr.emit(DVE, lambda e: e.memset(epst[:], EPS), writes=["eps"])
        tr.emit_dma(SP, lambda e: e.dma_start(out=pv[:], in_=pvec[:, :]), "sem_pv", writes=["pv"])

        seqn = [(li, ti) for li in range(L) for ti in range(ntiles)]

        def emit_cast(li, c):
            b0 = c * CPB
            b1 = min(NB, b0 + CPB)

            def f(e):
                src = wall[li, :, b0 * 128:b1 * 128].rearrange("p (a b) -> p a b", b=1024)
                dst = wbf[li, :, b0 * 128:b1 * 128].rearrange("p (a b) -> p a b", b=1024)
                return e.dma_start(out=dst, in_=src)
            rd = [("wbf", li, c - CWIN)] if c >= CWIN else []
            if li == 0 and c == 0:
                rd = xkeys(0) + [("xth", 0, 1), "pv"]
            if li > 0:
                rd = rd + [("h", 0)]
            tr.emit_dma(POOL, f, "sem_c%d_%d" % (li, c % CSEM), reads=rd, writes=[("wbf", li, c)])
        cast0 = {"next": 0}

        def cast0_upto(n):
            while cast0["next"] < min(n, NPIECE):
                emit_cast(0, cast0["next"])
                cast0["next"] += 1
        tr.emit(DVE, lambda e: e.memset(c_sb[:, :, TW:TW + 2], 0.0), writes=[("cpad",)])

        for li in range(L):
            tr.emit_dma(SP, lambda e, li=li: e.dma_start(out=wst_f[:], in_=wst[:, li * 1024:(li + 1) * 1024]),
                        "sem_wst", writes=["wst_f"])
            tr.emit_dma(SP, lambda e, li=li: e.dma_start(
                out=bsr_v, in_=bsr[:, li * 1024:(li + 1) * 1024].rearrange("p (a b) -> p a b", b=T)),
                "sem_bsr", writes=["bsr_sb", ("xt", 1, 0), ("xt", 1, 1)])
            tr.emit(DVE, lambda e, li=li: e.tensor_copy(out=wst_b[:, li * 1024:(li + 1) * 1024], in_=wst_f[:]),
                    reads=["wst_f"], writes=[("wst_b", li)])
            for hf in range(2):
                bank = next_bank()

                def fpe(e, li=li, hf=hf, bank=bank):
                    return e.matmul(ps[:, bank, :], ones[:],
                                    wst_b[:, li * 1024 + hf * 512: li * 1024 + (hf + 1) * 512],
                                    start=True, stop=True)
                tr.emit(PE, fpe, reads=["ones", ("wst_b", li)], writes=[("ps", bank)])
                for gg in range(4):
                    g = hf * 4 + gg

                    def fd(e, li=li, g=g, gg=gg, bank=bank):
                        return e.scalar_tensor_tensor(
                            out=Bsp[:, li * 8 + g, :], in0=ps[:, bank, gg * 128:(gg + 1) * 128],
                            scalar=pcol(li, 56, g), in1=bsr_v[:, g // 4, (g % 4) * 128:(g % 4 + 1) * 128],
                            op0=ALU.mult, op1=ALU.add)
                    tr.emit(DVE, fd, reads=[("ps", bank), "pv", "bsr_sb", ("xt", 1, 0), ("xt", 1, 1)],
                            writes=[("Bsp", li, g)])

        def src_dram(li):
            return xin if li == 0 else xs[li - 1]

        def dst_dram(li):
            return yout if li == L - 1 else xs[li]

        def xkeys(b):
            return [("xt", b, kc) for kc in range(KC)]

        def xd_keys(li, j):
            return [("xd", li, j, kc) for kc in range(KC)]

        def load_x(li, ti):
            b = (li * ntiles + ti) % 2
            sidx, tt = divmod(ti, tiles_per_seq)
            t0 = sidx * seq + tt * T
            src = src_dram(li)
            rdh = [k for j in (ti - 1, ti, ti + 1) if 0 <= j < ntiles for k in xd_keys(li, j)] if li > 0 else []
            for kc in range(KC):
                rd = [("xd", li, ti, kc)] if li > 0 else []
                tr.emit_dma(SP, lambda e, kc=kc: e.dma_start(out=xt[b][:, kc, 0:T], in_=src[:, kc, t0:t0 + T]),
                            "sem_xm%d_%d" % (b, kc), reads=rd, writes=[("xt", b, kc)])
            if tt > 0:
                tr.emit_dma(SP, lambda e: e.dma_start(out=xt[b][:, :, T:T + HALO], in_=src[:, :, t0 - HALO:t0]),
                            "sem_xl%d" % b, reads=rdh, writes=[("xth", b, 0)])
            else:
                tr.emit(DVE, lambda e: e.memset(xt[b][:, :, T:T + HALO], 0.0), writes=[("xth", b, 0)])
            if tt < tiles_per_seq - 1:
                tr.emit_dma(SP, lambda e: e.dma_start(out=xt[b][:, :, T + HALO:TW], in_=src[:, :, t0 + T:t0 + T + HALO]),
                            "sem_xr%d" % b, reads=rdh, writes=[("xth", b, 1)])
            else:
                tr.emit(DVE, lambda e: e.memset(xt[b][:, :, T + HALO:TW], 0.0), writes=[("xth", b, 1)])

        def store_x_chunk(li, ti, kc):
            b = (li * ntiles + ti) % 2
            sidx, tt = divmod(ti, tiles_per_seq)
            t0 = sidx * seq + tt * T
            dst = dst_dram(li)
            tr.emit_dma(SP, lambda e: e.dma_start(out=dst[:, kc, t0:t0 + T], in_=xt[b][:, kc, 0:T]),
                        "sem_st%d_%d" % (b, kc), reads=[("xt", b, kc)], writes=[("xd", li + 1, ti, kc)])

        SEGS = {"A1": (0, OFF_MIX), "A3": (OFF_MIX, OFF_FFN),
                "B1": (OFF_FFN, OFF_DOWN), "B2": (OFF_DOWN, NB)}
        NIT = len(seqn) + 1
        chunks = []
        seg_base = {}
        for it in range(NIT):
            for seg in ("A1", "B1", "B2", "A3"):
                tl = it if seg[0] == "A" else it - 1
                if tl < 0 or tl >= len(seqn):
                    continue
                seg_base[(it, seg)] = len(chunks)
                b0, b1 = SEGS[seg]
                for blk in range(b0, b1, CHB):
                    chunks.append((seqn[tl][0], blk, min(CHB, b1 - blk)))
        cur = {"it": 0, "in_v": False}
        state["wnext"] = 0

        def issue_wchunk(G):
            li_, blk0, nb = chunks[G]
            s = G % RING
            rd = [("wbf", li_, cc) for cc in range(blk0 // CPB, (blk0 + nb - 1) // CPB + 1)]
            tr.emit_dma(SP, lambda e: e.dma_start(out=wring[:, s, 0:nb * 128],
                                                  in_=wbf[li_, :, blk0 * 128:(blk0 + nb) * 128]),
                        "sem_w%d" % s, reads=rd, writes=[("wslot", s)])

        def touch(gset):
            gmin = seg_base[(cur["it"], "A1")] if cur["in_v"] else min(gset)
            assert max(gset) < gmin + RING
            limit = min(gmin + RING, len(chunks))
            while state["wnext"] < limit:
                issue_wchunk(state["wnext"])
                state["wnext"] += 1

        def wblk(bidx, n=1):
            for seg, (b0, b1) in SEGS.items():
                if b0 <= bidx < b1:
                    break
            c, o = divmod(bidx - b0, CHB)
            assert o + n <= CHB
            G = seg_base[(cur["it"], seg)] + c
            s = G % RING
            return wring[:, s, o * 128:(o + n) * 128], ("wslot", s), G

        def rms_squares(b, width_halo):
            wd = TW if width_halo else T
            for kc in range(KC):
                rd = [("xt", b, kc)] + ([("xth", b, 0), ("xth", b, 1)] if width_halo else [])
                tr.emit(ACT, lambda e, kc=kc: e.activation(out=sq[:, kc, 0:wd], in_=xt[b][:, kc, 0:wd], func=AF.Square),
                        reads=rd, writes=[("sq", kc)])

        def rms_rstd(width_halo):
            bank = next_bank()

            def fpe(e):
                for kc in range(KC):
                    ins = e.matmul(ps[:, bank, :], ones[:], sq[:, kc, 0:T], start=(kc == 0), stop=(kc == KC - 1))
                return ins
            tr.emit(PE, fpe, reads=["ones"] + [("sq", kc) for kc in range(KC)], writes=[("ps", bank)])
            tr.emit(ACT, lambda e: e.activation(out=rstd[:, 0:T], in_=ps[:, bank, :], func=AF.Sqrt,
                                                bias=epst[:, 0:1], scale=1.0 / D),
                    reads=[("ps", bank), "eps"], writes=["rstd_m"])
            if width_halo:
                hs = next_halo()

                def fpe2(e):
                    for kc in range(KC):
                        ins = e.matmul(ps[:, hs, 0:2 * HALO], ones[:], sq[:, kc, T:TW],
                                       start=(kc == 0), stop=(kc == KC - 1))
                    return ins
                tr.emit(PE, fpe2, reads=["ones"] + [("sq", kc) for kc in range(KC)], writes=[("ps", hs)])
                tr.emit(ACT, lambda e: e.activation(out=rstd[:, T:TW], in_=ps[:, hs, 0:2 * HALO],
                                                    func=AF.Sqrt, bias=epst[:, 0:1], scale=1.0 / D),
                        reads=[("ps", hs), "eps"], writes=["rstd_h"])
                tr.emit(DVE, lambda e: e.reciprocal(out=rstd[:, 0:TW], in_=rstd[:, 0:TW]),
                        reads=["rstd_m", "rstd_h"], writes=["rstd_m", "rstd_h"])
            else:
                tr.emit(DVE, lambda e: e.reciprocal(out=rstd[:, 0:T], in_=rstd[:, 0:T]),
                        reads=["rstd_m"], writes=["rstd_m"])

        def rms_stats(b, width_halo):
            rms_squares(b, width_halo)
            rms_rstd(width_halo)

        def norm_to_h_one(b, li, goff, width_halo, kc, hbuf, hname):
            wd = TW if width_halo else T
            rd = [("xt", b, kc), "rstd_m", "pv"] + ([("xth", b, 0), ("xth", b, 1), "rstd_h"] if width_halo else [])
            tr.emit(DVE, lambda e: e.scalar_tensor_tensor(
                out=hbuf[:, kc, 0:wd], in0=xt[b][:, kc, 0:wd], scalar=pcol(li, goff, kc), in1=rstd[:, 0:wd],
                op0=ALU.mult, op1=ALU.mult), reads=rd, writes=[(hname, kc)])

        def norm_to_h(b, li, goff, width_halo, hbuf, hname):
            for kc in range(KC):
                norm_to_h_one(b, li, goff, width_halo, kc, hbuf, hname)

        def hkeys(hname="h"):
            return [(hname, kc) for kc in range(KC)]

        def mm_group(bank_ap, blocks0, rhs_fn, nk, reads, wkey_out, stride=1):
            aps = []
            wkeys = set()
            chunks = set()
            for k in range(nk):
                ap, key, c = wblk(blocks0 + k * stride)
                aps.append(ap)
                wkeys.add(key)
                chunks.add(c)
            touch(chunks)

            def fpe(e):
                for k in range(nk):
                    ins = e.matmul(bank_ap, aps[k], rhs_fn(k), start=(k == 0), stop=(k == nk - 1))
                return ins
            tr.emit(PE, fpe, reads=list(wkeys) + list(reads), writes=[wkey_out])

        def stage_proj(b, li):
            for tc in range(4):
                banks = []
                for hf in range(2):
                    bank = next_bank()
                    banks.append(bank)
                    aps = []
                    wkeys = set()
                    chunks = set()
                    for k in range(KC):
                        ap, key, c = wblk(OFF_V + hf * 32 + k * 4, 4)
                        aps.append(ap)
                        wkeys.add(key)
                        chunks.add(c)
                    cur["in_v"] = True
                    touch(chunks)
                    cur["in_v"] = False

                    def fpe(e, tc=tc, bank=bank, aps=aps):
                        for k in range(KC):
                            ins = e.matmul(ps[:, bank, :], h[:, k, tc * 128:(tc + 1) * 128], aps[k],
                                           start=(k == 0), stop=(k == KC - 1))
                        return ins
                    tr.emit(PE, fpe, reads=list(wkeys) + hkeys(), writes=[("ps", bank)])
                    tr.emit(DVE, lambda e, tc=tc, hf=hf, bank=bank: e.bn_stats(out=vst[:, tc, hf, :], in_=ps[:, bank, :]),
                            reads=[("ps", bank)], writes=[("vst", tc, hf)])
                tr.emit(DVE, lambda e, tc=tc: e.bn_aggr(out=vmv[:, tc, :], in_=vst[:, tc, :, :].rearrange("p a b -> p (a b)")),
                        reads=[("vst", tc, 0), ("vst", tc, 1)], writes=[("vmv", tc)])
                tr.emit(ACT, lambda e, tc=tc: e.activation(out=vsc[:, tc, 0:1], in_=vmv[:, tc, 1:2], func=AF.Sqrt,
                                                           bias=epst[:, 0:1], scale=1.0),
                        reads=[("vmv", tc), "eps"], writes=[("vsc0", tc)])
                tr.emit(DVE, lambda e, tc=tc: e.reciprocal(out=vsc[:, tc, 0:1], in_=vsc[:, tc, 0:1]),
                        reads=[("vsc0", tc)], writes=[("vsc0", tc)])
                tr.emit(DVE, lambda e, tc=tc: e.scalar_tensor_tensor(
                    out=vsc[:, tc, 1:2], in0=vmv[:, tc, 0:1], scalar=-1.0, in1=vsc[:, tc, 0:1],
                    op0=ALU.mult, op1=ALU.mult), reads=[("vmv", tc), ("vsc0", tc)], writes=[("vsc1", tc)])
                for hf in range(2):
                    tr.emit(ACT, lambda e, tc=tc, hf=hf, bank=banks[hf]: e.activation(
                        out=vhat[:, tc, hf * 512:(hf + 1) * 512], in_=ps[:, bank, :], func=AF.Identity,
                        bias=vsc[:, tc, 1:2], scale=vsc[:, tc, 0:1]),
                        reads=[("ps", banks[hf]), ("vsc0", tc), ("vsc1", tc)], writes=[("vhat", tc, hf)])
            for m in range(KC):
                bank = next_bank()
                mm_group(ps[:, bank, :], OFF_U + m * 8, lambda k: h[:, k, 0:T], KC, hkeys(), ("ps", bank))
                tr.emit(ACT, lambda e, m=m, bank=bank: e.activation(out=u_sb[:, m, :], in_=ps[:, bank, :], func=AF.Copy),
                        reads=[("ps", bank)], writes=[("u", m)])
            def gv_pos(m):
                return OFF_GV + 16 * m + 8 * max(0, m - 3)

            def cu_pos(m):
                return gv_pos(m + 3) + 16 if m <= 4 else OFF_CU5 + (m - 5) * 8

            def conv_chunk(m):
                slot = m % NCST
                bank = next_bank()
                ap, key, G = wblk(cu_pos(m), 8)
                touch({G})

                def fpe(e):
                    for q in range(8):
                        for j in range(4):
                            ins = e.matmul(ps[32 * j:32 * j + 32, bank, :], ap[:, (q * 4 + j) * 32:(q * 4 + j + 1) * 32],
                                           cst[:, slot, j, 4 * q:4 * q + T], start=(q == 0), stop=(q == 7),
                                           tile_position=(0, 32 * j))
                    return ins
                tr.emit(PE, fpe, reads=[key, ("cst", slot)], writes=[("ps", bank)])
                tr.emit(ACT, lambda e: e.activation(out=y_sb[:, m, :], in_=ps[:, bank, :], func=AF.Identity,
                                                    bias=pcol(li, 24, m), scale=1.0),
                        reads=[("ps", bank), "pv"], writes=[("y", m)])
                tr.emit(ACT, lambda e: e.activation(out=sq[:, m, 0:T], in_=ps[:, bank, :], func=AF.Square,
                                                    bias=pcol(li, 24, m), scale=1.0),
                        reads=[("ps", bank), "pv"], writes=[("sq", m)])

            for m in range(KC):
                sl = m % 2
                cs_ = m % NCS
                bank_g = next_bank()
                hs_g = next_halo()
                mm_group(ps[:, bank_g, :], gv_pos(m), lambda k: h[:, k, 0:T], KC, hkeys(), ("ps", bank_g))
                mm_group(ps[:, hs_g, 0:2 * HALO], gv_pos(m), lambda k: h[:, k, T:TW], KC,
                         hkeys(), ("ps", hs_g))
                ts_ = next_tmp()
                tk = [("tmp", ts_, tc) for tc in range(4)]
                tr.emit(ACT, lambda e, ts_=ts_, bank=bank_g: e.activation(out=tmp[:, ts_, :], in_=ps[:, bank, :],
                                                                           func=AF.Sigmoid),
                        reads=[("ps", bank_g)], writes=tk)
                tr.emit(ACT, lambda e, sl=sl, hs=hs_g: e.activation(out=sgth[:, sl, 0:2 * HALO], in_=ps[:, hs, 0:2 * HALO],
                                                                     func=AF.Sigmoid),
                        reads=[("ps", hs_g)], writes=[("sgth", sl)])
                bank_v = next_bank()
                hs_v = next_halo()
                mm_group(ps[:, bank_v, :], gv_pos(m) + 8, lambda k: h[:, k, 0:T], KC, hkeys(), ("ps", bank_v))
                mm_group(ps[:, hs_v, 0:2 * HALO], gv_pos(m) + 8, lambda k: h[:, k, T:TW], KC,
                         hkeys(), ("ps", hs_v))
                tr.emit(DVE, lambda e, cs_=cs_, ts_=ts_, bank=bank_v: e.tensor_tensor(
                    out=c_sb[:, cs_, HALO:HALO + T], in0=ps[:, bank, :], in1=tmp[:, ts_, :], op=ALU.mult),
                    reads=[("ps", bank_v)] + tk, writes=[("c", cs_)])
                tr.emit(DVE, lambda e, cs_=cs_, sl=sl, hs=hs_v: e.tensor_tensor(
                    out=c_sb[:, cs_, 0:HALO], in0=ps[:, hs, 0:HALO], in1=sgth[:, sl, 0:HALO], op=ALU.mult),
                    reads=[("ps", hs_v), ("sgth", sl)], writes=[("ch0", cs_)])
                tr.emit(DVE, lambda e, cs_=cs_, sl=sl, hs=hs_v: e.tensor_tensor(
                    out=c_sb[:, cs_, HALO + T:TW], in0=ps[:, hs, HALO:2 * HALO],
                    in1=sgth[:, sl, HALO:2 * HALO], op=ALU.mult),
                    reads=[("ps", hs_v), ("sgth", sl)], writes=[("ch1", cs_)])
                slot = m % NCST
                fns = []
                ds = m % 4
                tr.emit_dma(SP, lambda e, cs_=cs_, ds=ds: e.dma_start(out=cdr[ds, :, :], in_=c_sb[:, cs_, :]),
                            "sem_cdw%d" % ds, reads=[("c", cs_), ("ch0", cs_), ("ch1", cs_), ("cpad",)],
                            writes=[("cdr", ds)])
                for s_ in range(4):
                    fns.append(lambda e, s_=s_, ds=ds, slot=slot: e.dma_start(
                        out=cst[32 * s_:32 * s_ + 32, slot, :, :],
                        in_=cdr[ds, :, s_:s_ + CSW].rearrange("(j c) x -> c j x", j=4)))
                tr.emit_dma_batch(SP, fns, "sem_cst%d" % slot, reads=[("cdr", ds)], writes=[("cst", slot)])
                if m >= 3:
                    conv_chunk(m - 3)
            for m in range(KC):
                for which, dstt, boff in ((0, siga, 8), (1, sigb, 16)):
                    bank = next_bank()
                    mm_group(ps[:, bank, :], OFF_GAB + m * 16 + which * 8, lambda k: h[:, k, 0:T], KC, hkeys(), ("ps", bank))
                    tr.emit(ACT, lambda e, m=m, bank=bank, dstt=dstt, boff=boff: e.activation(
                        out=dstt[:, m, :], in_=ps[:, bank, :], func=AF.Sigmoid, bias=pcol(li, boff, m), scale=1.0),
                        reads=[("ps", bank), "pv"], writes=[("sig", which, m)])
            for m in (5, 6, 7):
                conv_chunk(m)

        def stage_spatial(li):
            for g in range(KC):
                bank = next_bank()

                def fpe(e, g=g, bank=bank):
                    for tc in range(4):
                        ins = e.matmul(ps[:, bank, tc * 128:(tc + 1) * 128], vhat[:, tc, g * 128:(g + 1) * 128],
                                       wst_b[:, li * 1024 + g * 128: li * 1024 + (g + 1) * 128], start=True, stop=True)
                    return ins
                tr.emit(PE, fpe, reads=[("vhat", tc, g // 4) for tc in range(4)] + [("wst_b", li)], writes=[("ps", bank)])
                tslot = next_tmp()
                tr.emit(DVE, lambda e, g=g, bank=bank, tslot=tslot: e.scalar_tensor_tensor(
                    out=tmp[:, tslot, :].rearrange("p (a q) -> p a q", a=4),
                    in0=ps[:, bank, :].rearrange("p (a q) -> p a q", a=4),
                    scalar=pcol(li, 48, g),
                    in1=Bsp[:, li * 8 + g:li * 8 + g + 1, :].broadcast_to([128, 4, 128]),
                    op0=ALU.mult, op1=ALU.add),
                    reads=[("ps", bank), "pv", ("Bsp", li, g)], writes=[("tmp", tslot, tc) for tc in range(4)])
                tr.emit(DVE, lambda e, g=g, tslot=tslot: e.tensor_tensor(out=gm[:, g, :], in0=tmp[:, tslot, :],
                                                                           in1=u_sb[:, g, :], op=ALU.mult),
                        reads=[("tmp", tslot, tc) for tc in range(4)] + [("u", g)], writes=[("u", g)])

        def stage_conv_ln(li):
            b1 = next_bank()
            b2 = next_bank()

            def fpe1(e):
                for kc in range(KC):
                    ins = e.matmul(ps[:, b1, :], ones[:], y_sb[:, kc, :], start=(kc == 0), stop=(kc == KC - 1))
                return ins

            def fpe2(e):
                for kc in range(KC):
                    ins = e.matmul(ps[:, b2, :], ones[:], sq[:, kc, 0:T], start=(kc == 0), stop=(kc == KC - 1))
                return ins
            tr.emit(PE, fpe1, reads=["ones"] + [("y", m) for m in range(KC)], writes=[("ps", b1)])
            tr.emit(PE, fpe2, reads=["ones"] + [("sq", m) for m in range(KC)], writes=[("ps", b2)])
            tr.emit(DVE, lambda e: e.tensor_scalar(out=st_mean[:], in0=ps[:, b1, :], scalar1=1.0 / D, scalar2=None,
                                                   op0=ALU.mult), reads=[("ps", b1)], writes=["st_mean"])
            tr.emit(DVE, lambda e: e.tensor_tensor(out=st_mr[:], in0=st_mean[:], in1=st_mean[:], op=ALU.mult),
                    reads=["st_mean"], writes=["st_mr"])
            tr.emit(DVE, lambda e: e.scalar_tensor_tensor(out=st_rstd[:], in0=ps[:, b2, :], scalar=1.0 / D, in1=st_mr[:],
                                                          op0=ALU.mult, op1=ALU.subtract),
                    reads=[("ps", b2), "st_mr"], writes=["st_rstd"])
            tr.emit(ACT, lambda e: e.activation(out=st_rstd[:], in_=st_rstd[:], func=AF.Sqrt, bias=epst[:, 0:1], scale=1.0),
                    reads=["st_rstd", "eps"], writes=["st_rstd"])
            tr.emit(DVE, lambda e: e.reciprocal(out=st_rstd[:], in_=st_rstd[:]), reads=["st_rstd"], writes=["st_rstd"])
            tr.emit(DVE, lambda e: e.tensor_tensor(out=st_mr[:], in0=st_mean[:], in1=st_rstd[:], op=ALU.mult),
                    reads=["st_mean", "st_rstd"], writes=["st_mr"])
            for m in range(KC):
                tslot = next_tmp()
                tk = [("tmp", tslot, tc) for tc in range(4)]
                tr.emit(DVE, lambda e, m=m, tslot=tslot: e.tensor_tensor(out=tmp[:, tslot, :], in0=y_sb[:, m, :],
                                                                           in1=st_rstd[:], op=ALU.mult),
                        reads=[("y", m), "st_rstd"], writes=tk)
                tr.emit(DVE, lambda e, tslot=tslot: e.tensor_tensor(out=tmp[:, tslot, :], in0=tmp[:, tslot, :],
                                                                      in1=st_mr[:], op=ALU.subtract),
                        reads=tk + ["st_mr"], writes=tk)
                tr.emit(ACT, lambda e, m=m, tslot=tslot: e.activation(out=z_sb[:, m, :], in_=tmp[:, tslot, :], func=AF.Silu,
                                                                       bias=pcol(li, 40, m), scale=pcol(li, 32, m)),
                        reads=tk + ["pv"], writes=[("z", m)])

        def stage_mix_pairs(li):
            gk = [("u", k) for k in range(KC)]
            zk = [("z", k) for k in range(KC)]
            for m in range(KC):
                bank_b = next_bank()
                mm_group(ps[:, bank_b, :], OFF_MIX + m * 16, lambda k: gm[:, k, :], KC, gk, ("ps", bank_b))
                bank_a = next_bank()
                mm_group(ps[:, bank_a, :], OFF_MIX + m * 16 + 8, lambda k: z_sb[:, k, :], KC, zk, ("ps", bank_a))
                t1 = next_tmp()
                t2 = next_tmp()
                k1 = [("tmp", t1, tc) for tc in range(4)]
                k2 = [("tmp", t2, tc) for tc in range(4)]
                tr.emit(DVE, lambda e, m=m, bank=bank_b, t1=t1: e.tensor_tensor(out=tmp[:, t1, :], in0=ps[:, bank, :],
                                                                                 in1=sigb[:, m, :], op=ALU.mult),
                        reads=[("ps", bank_b), ("sig", 1, m)], writes=k1)
                tr.emit(DVE, lambda e, m=m, bank=bank_a, t2=t2: e.tensor_tensor(out=tmp[:, t2, :], in0=ps[:, bank, :],
                                                                                 in1=siga[:, m, :], op=ALU.mult),
                        reads=[("ps", bank_a), ("sig", 0, m)], writes=k2)
                tr.emit(DVE, lambda e, m=m, t1=t1, t2=t2: e.tensor_tensor(out=a_sb[:, m, :], in0=tmp[:, t1, :],
                                                                            in1=tmp[:, t2, :], op=ALU.add),
                        reads=k1 + k2, writes=[("a", m)])

        def stage_wo(b, li, after_each=None):
            mk = [("a", k) for k in range(KC)]
            for m in range(KC):
                bank = next_bank()
                mm_group(ps[:, bank, :], OFF_WO + m * 8, lambda k: a_sb[:, k, :], KC, mk, ("ps", bank))
                tr.emit(DVE, lambda e, m=m, bank=bank: e.tensor_tensor(out=xt[b][:, m, 0:T], in0=ps[:, bank, :],
                                                                        in1=xt[b][:, m, 0:T], op=ALU.add),
                        reads=[("ps", bank), ("xt", b, m)], writes=[("xt", b, m)])
                if after_each is not None:
                    after_each(m)

        def stage_ffn_up(li):
            for j in range(NJ):
                bank_g = next_bank()
                mm_group(ps[:, bank_g, :], OFF_FFN + j * 16, lambda k: h2[:, k, :], KC, hkeys("h2"), ("ps", bank_g))
                bank_u = next_bank()
                mm_group(ps[:, bank_u, :], OFF_FFN + j * 16 + 8, lambda k: h2[:, k, :], KC, hkeys("h2"), ("ps", bank_u))
                tslot = next_tmp()
                tk = [("tmp", tslot, tc) for tc in range(4)]
                tr.emit(ACT, lambda e, bank=bank_g, tslot=tslot: e.activation(out=tmp[:, tslot, :], in_=ps[:, bank, :],
                                                                               func=AF.Silu),
                        reads=[("ps", bank_g)], writes=tk)
                tr.emit(DVE, lambda e, j=j, bank=bank_u, tslot=tslot: e.tensor_tensor(
                    out=a_sb[:, j, :], in0=ps[:, bank, :], in1=tmp[:, tslot, :], op=ALU.mult),
                    reads=[("ps", bank_u)] + tk, writes=[("a", j)])

        def stage_ffn_down(b, li, after_each=None):
            ak = [("a", j) for j in range(NJ)]
            for m in range(KC):
                bank = next_bank()
                mm_group(ps[:, bank, :], OFF_DOWN + m * NJ, lambda k: a_sb[:, k, :], NJ, ak, ("ps", bank))
                tr.emit(DVE, lambda e, m=m, bank=bank: e.tensor_tensor(out=xt[b][:, m, 0:T], in0=ps[:, bank, :],
                                                                        in1=xt[b][:, m, 0:T], op=ALU.add),
                        reads=[("ps", bank), ("xt", b, m)], writes=[("xt", b, m)])
                if after_each is not None:
                    after_each(m)

        def stage_final_norm(b, after_each=None):
            rms_stats(b, False)
            for kc in range(KC):
                c = L * NPL + kc
                tr.emit(DVE, lambda e, kc=kc, c=c: e.scalar_tensor_tensor(
                    out=xt[b][:, kc, 0:T], in0=xt[b][:, kc, 0:T], scalar=pv[:, c:c + 1], in1=rstd[:, 0:T],
                    op0=ALU.mult, op1=ALU.mult), reads=[("xt", b, kc), "rstd_m", "pv"], writes=[("xt", b, kc)])
                if after_each is not None:
                    after_each(kc)

        NTL = len(seqn)

        def emit_casts_for(n):
            li, ti = seqn[n]
            if li + 1 < L:
                per = (NPIECE + ntiles - 1) // ntiles
                for c in range(ti * per, min((ti + 1) * per, NPIECE)):
                    emit_cast(li + 1, c)

        load_x(*seqn[0])
        cast0_upto(NPIECE)
        rms_stats(0, True)
        norm_to_h(0, seqn[0][0], 0, True, h, "h")
        for it in range(NTL + 1):
            cur["it"] = it
            hasA = it < NTL
            hasB = it >= 1
            if hasA:
                li, ti = seqn[it]
                b = it % 2
                emit_casts_for(it)
            if hasB:
                lb, tb = seqn[it - 1]
                bb = (it - 1) % 2
            if hasA:
                stage_proj(b, li)
                if it == 0:
                    cast0_upto(8)
            if hasB:
                stage_ffn_up(lb)
            if hasA:
                stage_spatial(li)
                stage_conv_ln(li)
            if hasB:
                st_hook = lambda kc, lb=lb, tb=tb: store_x_chunk(lb, tb, kc)
                if final_norm and lb == L - 1:
                    stage_ffn_down(bb, lb)
                    stage_final_norm(bb, st_hook)
                else:
                    stage_ffn_down(bb, lb, st_hook)
            if it + 1 < NTL:
                load_x(*seqn[it + 1])
                rms_squares((it + 1) % 2, True)
            if hasA:
                stage_mix_pairs(li)
                if it == 0:
                    cast0_upto(NPIECE)
            if it + 1 < NTL:
                rms_rstd(True)
            if hasA:
                nxt = (it + 1) % 2
                lnx = seqn[it + 1][0] if it + 1 < NTL else None
                hook = (lambda m: norm_to_h_one(nxt, lnx, 0, True, m, h, "h")) if it + 1 < NTL else None
                stage_wo(b, li, hook)
                rms_stats(b, False)
                norm_to_h(b, li, 64, False, h2, "h2")
        tr.emit_wait_only(SP, [k for ti in range(ntiles) for k in xd_keys(L, ti)])

        semkeys = set()
        for eng in engines:
            for waits, fn, sk, inc in eng.prog:
                for s, v in waits:
                    semkeys.add(s)
                if sk:
                    semkeys.add(sk)
        sems = {k: es.enter_context(nc.semaphore(k)) for k in sorted(semkeys)}

        def replay(eng, handle):
            for waits, fn, sk, inc in eng.prog:
                for s, v in waits:
                    handle.wait_ge(sems[s], v)
                if fn is not None:
                    ins = fn(handle)
                    ins.then_inc(sems[sk], inc)

        block = es.enter_context(nc.Block())
        block.tensor(lambda e: replay(PE, e))
        block.scalar(lambda e: replay(ACT, e))
        block.vector(lambda e: replay(DVE, e))
        block.gpsimd(lambda e: replay(POOL, e))
        block.sync(lambda e: replay(SP, e))
    return nc


def _prep_params(inputs, layers, with_final):
    L = len(layers)
    wall = np.stack([_layer_blocks(inputs["w_in"][l], inputs["conv_w"][l], inputs["w_conv_out"][l],
                                   inputs["w_sgu_out"][l], inputs["w_o"][l], inputs["w_ffn_gate"][l],
                                   inputs["w_ffn_up"][l], inputs["w_ffn_down"][l]) for l in layers])
    pvec = np.zeros((128, NPL * L + 8), np.float32)
    wst = np.zeros((128, L * 1024), np.float32)
    bsr = np.zeros((128, L * 1024), np.float32)
    for i, l in enumerate(layers):
        o = i * NPL
        pvec[:, o + 0:o + 8] = _vec8(inputs["norm_mix"][l])
        pvec[:, o + 8:o + 16] = _vec8(inputs["gate_bias"][l][:D])
        pvec[:, o + 16:o + 24] = _vec8(inputs["gate_bias"][l][D:])
        pvec[:, o + 24:o + 32] = _vec8(inputs["conv_b"][l])
        pvec[:, o + 32:o + 40] = _vec8(inputs["conv_ln_g"][l])
        pvec[:, o + 40:o + 48] = _vec8(inputs["conv_ln_b"][l])
        pvec[:, o + 48:o + 56] = _vec8(inputs["sgu_ln_g"][l])
        pvec[:, o + 56:o + 64] = _vec8(inputs["sgu_ln_b"][l])
        pvec[:, o + 64:o + 72] = _vec8(inputs["norm_ffn"][l])
        wst[:, i * 1024:(i + 1) * 1024] = inputs["w_spatial"][l].transpose(2, 0, 1).reshape(128, 1024)
        bsr[:, i * 1024:(i + 1) * 1024] = np.broadcast_to(inputs["b_spatial"][l].reshape(1, 1024), (128, 1024))
    if with_final:
        pvec[:, NPL * L:NPL * L + 8] = _vec8(inputs["norm_final"])
    return wall, pvec, wst, bsr


def _to_fm(x, ncores):
    B, S, _ = x.shape
    nseq = B // ncores
    xr = x.reshape(ncores, nseq * S, 8, 128)
    return np.ascontiguousarray(xr.transpose(0, 3, 2, 1))


def _from_fm(y, B, S):
    ncores = y.shape[0]
    return np.ascontiguousarray(y.transpose(0, 3, 2, 1)).reshape(B, S, D)


_PROG_CACHE = {}


def _get_prog(n_layers, nseq, seq, final_norm):
    key = (n_layers, nseq, seq, final_norm)
    if key not in _PROG_CACHE:
        _PROG_CACHE[key] = build_program(n_layers, nseq, seq, final_norm)
    return _PROG_CACHE[key]


def run_layers(xfm, inputs, layers, with_final, ncores, nseq, seq):
    wall, pvec, wst, bsr = _prep_params(inputs, layers, with_final)
    nc = _get_prog(len(layers), nseq, seq, with_final)
    in_maps = [{"xin": xfm[c], "wall": wall, "pvec": pvec, "wst": wst, "bsr": bsr} for c in range(ncores)]
    res = run_bass_kernel_spmd(nc, in_maps, core_ids=list(range(ncores)))
    return np.stack([np.asarray(r["yout"]) for r in res.results])


FUSED = True


def kernel(**inputs):
    inputs = {k: np.asarray(v, np.float32) for k, v in inputs.items()}
    x = inputs["x"]
    B, S, _ = x.shape
    depth = inputs["w_in"].shape[0]
    nseq = B // NCORES
    xfm = _to_fm(x, NCORES)
    if FUSED:
        y = run_layers(xfm, inputs, list(range(depth)), True, NCORES, nseq, S)
    else:
        y = xfm
        for l in range(depth):
            y = run_layers(y, inputs, [l], l == depth - 1, NCORES, nseq, S)
    return _from_fm(y, B, S).astype(np.float32)
```

```python
import numpy as np
from contextlib import ExitStack
import concourse.bass as bass
import concourse.mybir as mybir
from concourse.bass_utils import run_bass_kernel_spmd

F32 = mybir.dt.float32
BF16 = mybir.dt.bfloat16
AF = mybir.ActivationFunctionType
ALU = mybir.AluOpType

D = 1024
KC = 8
DFF = 2816
NJ = 22
CW = 31
HALO = 15
T = 512
TW = T + 2 * HALO
EPS = 1e-6
NCORES = 8

OFF_V = 0
OFF_U = 64
OFF_GV = 128
GVLEN = 128 + 40
OFF_GAB = OFF_GV + GVLEN
OFF_CU5 = OFF_GAB + 128
OFF_MIX = OFF_CU5 + 24
OFF_WO = OFF_MIX + 128
OFF_FFN = OFF_WO + 64
OFF_DOWN = OFF_FFN + 352
NB = OFF_DOWN + 176
CHB = 32
RING = 5
CPB = 16
CWIN = 3
CSEM = 6
NPIECE = (NB + CPB - 1) // CPB
NCS = 2
NCST = 4
CSW = 540
NPL = 72


def _blk(W):
    K, M = W.shape
    return W.reshape(K // 128, 128, M // 128, 128).transpose(1, 0, 2, 3)


def _layer_blocks(w_in, conv_w, w_conv_out, w_sgu_out, w_o, w_gate, w_up, w_down):
    out = np.zeros((128, NB, 128), np.float32)
    bv = _blk(w_in[:, 3072:4096])
    out[:, OFF_V:OFF_V + 64] = bv.reshape(128, 8, 2, 4, 128).transpose(0, 2, 1, 3, 4).reshape(128, 64, 128)
    out[:, OFF_U:OFF_U + 64] = _blk(w_in[:, 2048:3072]).transpose(0, 2, 1, 3).reshape(128, 64, 128)
    bg = _blk(w_in[:, 1024:2048]).transpose(0, 2, 1, 3)
    bva = _blk(w_in[:, 0:1024]).transpose(0, 2, 1, 3)
    wpad = np.zeros((32, 1024), np.float32)
    wpad[:CW] = conv_w.reshape(CW, 1024)
    w5 = wpad.reshape(8, 4, 8, 4, 32)
    cu = np.zeros((8, 4, 32, 8, 4, 32), np.float32)
    ii = np.arange(32)
    cu[:, :, ii, :, :, ii] = w5.transpose(4, 2, 1, 0, 3)
    cu = cu.reshape(8, 128, 8, 128).transpose(1, 0, 2, 3)
    cu = cu.reshape(128, 8, 1024).reshape(128, 8, 8, 128)
    pos = OFF_GV
    for m in range(8):
        out[:, pos:pos + 8] = bg[:, m]
        out[:, pos + 8:pos + 16] = bva[:, m]
        pos += 16
        if m >= 3:
            out[:, pos:pos + 8] = cu[:, m - 3]
            pos += 8
    assert pos == OFF_GAB
    for m in (5, 6, 7):
        out[:, OFF_CU5 + (m - 5) * 8:OFF_CU5 + (m - 4) * 8] = cu[:, m]
    ba = _blk(w_in[:, 4096:5120]).transpose(0, 2, 1, 3)
    bb = _blk(w_in[:, 5120:6144]).transpose(0, 2, 1, 3)
    out[:, OFF_GAB:OFF_GAB + 128] = np.stack([ba, bb], axis=2).reshape(128, 128, 128)
    bs = _blk(w_sgu_out).transpose(0, 2, 1, 3)
    bc = _blk(w_conv_out).transpose(0, 2, 1, 3)
    out[:, OFF_MIX:OFF_MIX + 128] = np.stack([bs, bc], axis=2).reshape(128, 128, 128)
    out[:, OFF_WO:OFF_WO + 64] = _blk(w_o).transpose(0, 2, 1, 3).reshape(128, 64, 128)
    fg = _blk(w_gate).transpose(0, 2, 1, 3)
    fu = _blk(w_up).transpose(0, 2, 1, 3)
    out[:, OFF_FFN:OFF_FFN + 352] = np.stack([fg, fu], axis=2).reshape(128, 352, 128)
    out[:, OFF_DOWN:OFF_DOWN + 176] = _blk(w_down).transpose(0, 2, 1, 3).reshape(128, 176, 128)
    return out.reshape(128, NB * 128)


def _vec8(v):
    return np.ascontiguousarray(np.asarray(v, np.float32).reshape(8, 128).T)


class _Eng:
    def __init__(self, name, skip_self=False):
        self.name = name
        self.semkey = "prog_" + name
        self.count = 0
        self.waited = {}
        self.prog = []
        self.skip_self = skip_self


class _Tracker:
    def __init__(self):
        self.last_w = {}
        self.readers = {}
        self.dma_counts = {}

    def _deps(self, eng, reads, writes):
        evs = {}

        def add(s, v):
            if evs.get(s, 0) < v:
                evs[s] = v
        for k in reads:
            ev = self.last_w.get(k)
            assert ev is not None, "read of never-written key %r" % (k,)
            add(*ev)
        for k in writes:
            ev = self.last_w.get(k)
            if ev is not None:
                add(*ev)
            for s, v in self.readers.get(k, {}).items():
                add(s, v)
        out = []
        for s, v in evs.items():
            if eng.skip_self and s == eng.semkey:
                continue
            if eng.waited.get(s, 0) >= v:
                continue
            eng.waited[s] = v
            out.append((s, v))
        return out

    def _commit(self, ev, reads, writes):
        for k in writes:
            self.last_w[k] = ev
            self.readers[k] = {}
        for k in reads:
            r = self.readers.setdefault(k, {})
            if r.get(ev[0], 0) < ev[1]:
                r[ev[0]] = ev[1]

    def emit(self, eng, fn, reads=(), writes=()):
        waits = self._deps(eng, reads, writes)
        eng.count += 1
        ev = (eng.semkey, eng.count)
        eng.prog.append((waits, fn, eng.semkey, 1))
        self._commit(ev, reads, writes)
        return ev

    def emit_dma(self, eng, fn, semkey, reads=(), writes=()):
        waits = self._deps(eng, reads, writes)
        c = self.dma_counts.get(semkey, 0) + 16
        self.dma_counts[semkey] = c
        ev = (semkey, c)
        eng.prog.append((waits, fn, semkey, 16))
        self._commit(ev, reads, writes)
        return ev

    def emit_dma_batch(self, eng, fns, semkey, reads=(), writes=()):
        waits = self._deps(eng, reads, writes)
        waits = [(s_, v_) for (s_, v_) in waits if s_ != semkey]
        c = self.dma_counts.get(semkey, 0) + 16 * len(fns)
        self.dma_counts[semkey] = c
        ev = (semkey, c)
        for i, fn in enumerate(fns):
            eng.prog.append((waits if i == 0 else [], fn, semkey, 16))
        self._commit(ev, reads, writes)
        return ev

    def emit_wait_only(self, eng, reads):
        waits = self._deps(eng, reads, ())
        eng.prog.append((waits, None, None, 0))


def build_program(n_layers, nseq, seq, final_norm):
    ntok = nseq * seq
    tiles_per_seq = seq // T
    ntiles = nseq * tiles_per_seq
    L = n_layers

    nc = bass.Bass("TRN2", target_bir_lowering=False)
    xin = nc.dram_tensor("xin", [128, KC, ntok], F32, kind="ExternalInput").ap()
    wall = nc.dram_tensor("wall", [L, 128, NB * 128], F32, kind="ExternalInput").ap()
    pvec = nc.dram_tensor("pvec", [128, NPL * L + 8], F32, kind="ExternalInput").ap()
    wst = nc.dram_tensor("wst", [128, L * 8 * 128], F32, kind="ExternalInput").ap()
    bsr = nc.dram_tensor("bsr", [128, L * 8 * 128], F32, kind="ExternalInput").ap()
    yout = nc.dram_tensor("yout", [128, KC, ntok], F32, kind="ExternalOutput").ap()
    wbf = nc.dram_tensor("wbf", [L, 128, NB * 128], BF16, kind="Internal").ap()
    cdr = nc.dram_tensor("cdr", [4, 128, TW + 2], BF16, kind="Internal").ap()
    xs = [nc.dram_tensor("xs%d" % i, [128, KC, ntok], F32, kind="Internal").ap() for i in range(max(L - 1, 0))]

    tr = _Tracker()
    PE = _Eng("pe", skip_self=True)
    ACT = _Eng("act")
    DVE = _Eng("dve")
    POOL = _Eng("pool")
    SP = _Eng("sp", skip_self=True)
    engines = [PE, ACT, DVE, POOL, SP]

    with ExitStack() as es:
        def sb(name, shape, dt):
            return es.enter_context(nc.sbuf_tensor(name, shape, dt))

        xt = [sb("xt%d" % b, [128, KC, TW], F32) for b in range(2)]
        sq = sb("sq", [128, KC, TW], BF16)
        h = sb("h", [128, KC, TW], BF16)
        h2 = sb("h2", [128, KC, T], BF16)
        rstd = sb("rstd", [128, TW], F32)
        c_sb = sb("c_sb", [128, NCS, TW + 2], BF16)
        cst = sb("cst", [128, NCST, 4, CSW], BF16)
        y_sb = sb("y_sb", [128, KC, T], BF16)
        vz = sb("vz", [128, 4 * D], BF16)
        vhat = vz[:, :].rearrange("p (a b) -> p a b", b=D)
        z_sb = vz[:, :].rearrange("p (a b) -> p a b", b=T)
        a_sb = sb("a_sb", [128, NJ, T], BF16)
        vst = sb("vst", [128, 4, 2, 6], F32)
        vmv = sb("vmv", [128, 4, 2], F32)
        vsc = sb("vsc", [128, 4, 2], F32)
        u_sb = sb("u_sb", [128, KC, T], BF16)
        sgth = sb("sgth", [128, 2, 32], F32)
        siga = sb("siga", [128, KC, T], BF16)
        sigb = sb("sigb", [128, KC, T], BF16)
        st_mean = sb("st_mean", [128, T], F32)
        st_rstd = sb("st_rstd", [128, T], F32)
        st_mr = sb("st_mr", [128, T], F32)
        NTMP = 3
        tmp = sb("tmp", [128, NTMP, T], F32)
        gm = u_sb
        wring = sb("wring", [128, RING, CHB * 128], BF16)
        ones = sb("ones", [128, 128], BF16)
        epst = sb("epst", [128, 1], F32)
        pv = sb("pv", [128, NPL * L + 8], F32)
        wst_f = tmp[:, 0:2, :].rearrange("p a b -> p (a b)")
        bsr_v = xt[1][:, 0:2, 0:T]
        wst_b = sb("wst_b", [128, L * 8 * 128], BF16)
        Bsp = sb("Bsp", [128, L * 8, 128], F32)
        ps = es.enter_context(nc.psum_tensor("ps", [128, 8, 512], F32))

        NRING_PS = 6
        state = {"psb": 0, "tmp": 0, "halo": 0, "wfill": 0}

        def next_bank():
            b = state["psb"]
            state["psb"] = (b + 1) % NRING_PS
            return b

        def next_tmp():
            t = state["tmp"]
            state["tmp"] = (t + 1) % NTMP
            return t

        def next_halo():
            s = state["halo"]
            state["halo"] = (s + 1) % 2
            return 6 + s

        def pcol(li, off, m):
            c = li * NPL + off + m
            return pv[:, c:c + 1]

        tr.emit(DVE, lambda e: e.memset(ones[:], 1.0), writes=["ones"])
        tr.emit(DVE, lambda e: e.memset(epst[:], EPS), writes=["eps"])
        tr.emit_dma(SP, lambda e: e.dma_start(out=pv[:], in_=pvec[:, :]), "sem_pv", writes=["pv"])

        seqn = [(li, ti) for li in range(L) for ti in range(ntiles)]

        def emit_cast(li, c):
            b0 = c * CPB
            b1 = min(NB, b0 + CPB)

            def f(e):
                src = wall[li, :, b0 * 128:b1 * 128].rearrange("p (a b) -> p a b", b=1024)
                dst = wbf[li, :, b0 * 128:b1 * 128].rearrange("p (a b) -> p a b", b=1024)
                return e.dma_start(out=dst, in_=src)
            rd = [("wbf", li, c - CWIN)] if c >= CWIN else []
            if li > 0:
                rd = rd + [("h", 0)]
            tr.emit_dma(POOL, f, "sem_c%d_%d" % (li, c % CSEM), reads=rd, writes=[("wbf", li, c)])
        cast0 = {"next": 0}

        def cast0_upto(n):
            while cast0["next"] < min(n, NPIECE):
                emit_cast(0, cast0["next"])
                cast0["next"] += 1
        tr.emit(DVE, lambda e: e.memset(c_sb[:, :, TW:TW + 2], 0.0), writes=[("cpad",)])

        for li in range(L):
            tr.emit_dma(SP, lambda e, li=li: e.dma_start(out=wst_f[:], in_=wst[:, li * 1024:(li + 1) * 1024]),
                        "sem_wst", writes=["wst_f"])
            tr.emit_dma(SP, lambda e, li=li: e.dma_start(
                out=bsr_v, in_=bsr[:, li * 1024:(li + 1) * 1024].rearrange("p (a b) -> p a b", b=T)),
                "sem_bsr", writes=["bsr_sb", ("xt", 1, 0), ("xt", 1, 1)])
            tr.emit(DVE, lambda e, li=li: e.tensor_copy(out=wst_b[:, li * 1024:(li + 1) * 1024], in_=wst_f[:]),
                    reads=["wst_f"], writes=[("wst_b", li)])
            for hf in range(2):
                bank = next_bank()

                def fpe(e, li=li, hf=hf, bank=bank):
                    return e.matmul(ps[:, bank, :], ones[:],
                                    wst_b[:, li * 1024 + hf * 512: li * 1024 + (hf + 1) * 512],
                                    start=True, stop=True)
                tr.emit(PE, fpe, reads=["ones", ("wst_b", li)], writes=[("ps", bank)])
                for gg in range(4):
                    g = hf * 4 + gg

                    def fd(e, li=li, g=g, gg=gg, bank=bank):
                        return e.scalar_tensor_tensor(
                            out=Bsp[:, li * 8 + g, :], in0=ps[:, bank, gg * 128:(gg + 1) * 128],
                            scalar=pcol(li, 56, g), in1=bsr_v[:, g // 4, (g % 4) * 128:(g % 4 + 1) * 128],
                            op0=ALU.mult, op1=ALU.add)
                    tr.emit(DVE, fd, reads=[("ps", bank), "pv", "bsr_sb", ("xt", 1, 0), ("xt", 1, 1)],
                            writes=[("Bsp", li, g)])

        def src_dram(li):
            return xin if li == 0 else xs[li - 1]

        def dst_dram(li):
            return yout if li == L - 1 else xs[li]

        def xkeys(b):
            return [("xt", b, kc) for kc in range(KC)]

        def xd_keys(li, j):
            return [("xd", li, j, kc) for kc in range(KC)]

        def load_x(li, ti):
            b = (li * ntiles + ti) % 2
            sidx, tt = divmod(ti, tiles_per_seq)
            t0 = sidx * seq + tt * T
            src = src_dram(li)
            rdh = [k for j in (ti - 1, ti, ti + 1) if 0 <= j < ntiles for k in xd_keys(li, j)] if li > 0 else []
            for kc in range(KC):
                rd = [("xd", li, ti, kc)] if li > 0 else []
                tr.emit_dma(SP, lambda e, kc=kc: e.dma_start(out=xt[b][:, kc, 0:T], in_=src[:, kc, t0:t0 + T]),
                            "sem_xm%d_%d" % (b, kc), reads=rd, writes=[("xt", b, kc)])
            if tt > 0:
                tr.emit_dma(SP, lambda e: e.dma_start(out=xt[b][:, :, T:T + HALO], in_=src[:, :, t0 - HALO:t0]),
                            "sem_xl%d" % b, reads=rdh, writes=[("xth", b, 0)])
            else:
                tr.emit(DVE, lambda e: e.memset(xt[b][:, :, T:T + HALO], 0.0), writes=[("xth", b, 0)])
            if tt < tiles_per_seq - 1:
                tr.emit_dma(SP, lambda e: e.dma_start(out=xt[b][:, :, T + HALO:TW], in_=src[:, :, t0 + T:t0 + T + HALO]),
                            "sem_xr%d" % b, reads=rdh, writes=[("xth", b, 1)])
            else:
                tr.emit(DVE, lambda e: e.memset(xt[b][:, :, T + HALO:TW], 0.0), writes=[("xth", b, 1)])

        def store_x_chunk(li, ti, kc):
            b = (li * ntiles + ti) % 2
            sidx, tt = divmod(ti, tiles_per_seq)
            t0 = sidx * seq + tt * T
            dst = dst_dram(li)
            tr.emit_dma(SP, lambda e: e.dma_start(out=dst[:, kc, t0:t0 + T], in_=xt[b][:, kc, 0:T]),
                        "sem_st%d_%d" % (b, kc), reads=[("xt", b, kc)], writes=[("xd", li + 1, ti, kc)])

        SEGS = {"A1": (0, OFF_MIX), "A3": (OFF_MIX, OFF_FFN),
                "B1": (OFF_FFN, OFF_DOWN), "B2": (OFF_DOWN, NB)}
        NIT = len(seqn) + 1
        chunks = []
        seg_base = {}
        for it in range(NIT):
            for seg in ("A1", "B1", "B2", "A3"):
                tl = it if seg[0] == "A" else it - 1
                if tl < 0 or tl >= len(seqn):
                    continue
                seg_base[(it, seg)] = len(chunks)
                b0, b1 = SEGS[seg]
                for blk in range(b0, b1, CHB):
                    chunks.append((seqn[tl][0], blk, min(CHB, b1 - blk)))
        cur = {"it": 0, "in_v": False}
        state["wnext"] = 0

        def issue_wchunk(G):
            li_, blk0, nb = chunks[G]
            s = G % RING
            rd = [("wbf", li_, cc) for cc in range(blk0 // CPB, (blk0 + nb - 1) // CPB + 1)]
            tr.emit_dma(SP, lambda e: e.dma_start(out=wring[:, s, 0:nb * 128],
                                                  in_=wbf[li_, :, blk0 * 128:(blk0 + nb) * 128]),
                        "sem_w%d" % s, reads=rd, writes=[("wslot", s)])

        def touch(gset):
            gmin = seg_base[(cur["it"], "A1")] if cur["in_v"] else min(gset)
            assert max(gset) < gmin + RING
            limit = min(gmin + RING, len(chunks))
            while state["wnext"] < limit:
                issue_wchunk(state["wnext"])
                state["wnext"] += 1

        def wblk(bidx, n=1):
            for seg, (b0, b1) in SEGS.items():
                if b0 <= bidx < b1:
                    break
            c, o = divmod(bidx - b0, CHB)
            assert o + n <= CHB
            G = seg_base[(cur["it"], seg)] + c
            s = G % RING
            return wring[:, s, o * 128:(o + n) * 128], ("wslot", s), G

        def rms_squares(b, width_halo):
            wd = TW if width_halo else T
            for kc in range(KC):
                rd = [("xt", b, kc)] + ([("xth", b, 0), ("xth", b, 1)] if width_halo else [])
                tr.emit(ACT, lambda e, kc=kc: e.activation(out=sq[:, kc, 0:wd], in_=xt[b][:, kc, 0:wd], func=AF.Square),
                        reads=rd, writes=[("sq", kc)])

        def rms_rstd(width_halo):
            bank = next_bank()

            def fpe(e):
                for kc in range(KC):
                    ins = e.matmul(ps[:, bank, :], ones[:], sq[:, kc, 0:T], start=(kc == 0), stop=(kc == KC - 1))
                return ins
            tr.emit(PE, fpe, reads=["ones"] + [("sq", kc) for kc in range(KC)], writes=[("ps", bank)])
            tr.emit(ACT, lambda e: e.activation(out=rstd[:, 0:T], in_=ps[:, bank, :], func=AF.Sqrt,
                                                bias=epst[:, 0:1], scale=1.0 / D),
                    reads=[("ps", bank), "eps"], writes=["rstd_m"])
            if width_halo:
                hs = next_halo()

                def fpe2(e):
                    for kc in range(KC):
                        ins = e.matmul(ps[:, hs, 0:2 * HALO], ones[:], sq[:, kc, T:TW],
                                       start=(kc == 0), stop=(kc == KC - 1))
                    return ins
                tr.emit(PE, fpe2, reads=["ones"] + [("sq", kc) for kc in range(KC)], writes=[("ps", hs)])
                tr.emit(ACT, lambda e: e.activation(out=rstd[:, T:TW], in_=ps[:, hs, 0:2 * HALO],
                                                    func=AF.Sqrt, bias=epst[:, 0:1], scale=1.0 / D),
                        reads=[("ps", hs), "eps"], writes=["rstd_h"])
                tr.emit(DVE, lambda e: e.reciprocal(out=rstd[:, 0:TW], in_=rstd[:, 0:TW]),
                        reads=["rstd_m", "rstd_h"], writes=["rstd_m", "rstd_h"])
            else:
                tr.emit(DVE, lambda e: e.reciprocal(out=rstd[:, 0:T], in_=rstd[:, 0:T]),
                        reads=["rstd_m"], writes=["rstd_m"])

        def rms_stats(b, width_halo):
            rms_squares(b, width_halo)
            rms_rstd(width_halo)

        def norm_to_h_one(b, li, goff, width_halo, kc, hbuf, hname):
            wd = TW if width_halo else T
            rd = [("xt", b, kc), "rstd_m", "pv"] + ([("xth", b, 0), ("xth", b, 1), "rstd_h"] if width_halo else [])
            tr.emit(DVE, lambda e: e.scalar_tensor_tensor(
                out=hbuf[:, kc, 0:wd], in0=xt[b][:, kc, 0:wd], scalar=pcol(li, goff, kc), in1=rstd[:, 0:wd],
                op0=ALU.mult, op1=ALU.mult), reads=rd, writes=[(hname, kc)])

        def norm_to_h(b, li, goff, width_halo, hbuf, hname):
            for kc in range(KC):
                norm_to_h_one(b, li, goff, width_halo, kc, hbuf, hname)

        def hkeys(hname="h"):
            return [(hname, kc) for kc in range(KC)]

        def mm_group(bank_ap, blocks0, rhs_fn, nk, reads, wkey_out, stride=1):
            aps = []
            wkeys = set()
            chunks = set()
            for k in range(nk):
                ap, key, c = wblk(blocks0 + k * stride)
                aps.append(ap)
                wkeys.add(key)
                chunks.add(c)
            touch(chunks)

            def fpe(e):
                for k in range(nk):
                    ins = e.matmul(bank_ap, aps[k], rhs_fn(k), start=(k == 0), stop=(k == nk - 1))
                return ins
            tr.emit(PE, fpe, reads=list(wkeys) + list(reads), writes=[wkey_out])

        def stage_proj(b, li):
            for tc in range(4):
                banks = []
                for hf in range(2):
                    bank = next_bank()
                    banks.append(bank)
                    aps = []
                    wkeys = set()
                    chunks = set()
                    for k in range(KC):
                        ap, key, c = wblk(OFF_V + hf * 32 + k * 4, 4)
                        aps.append(ap)
                        wkeys.add(key)
                        chunks.add(c)
                    cur["in_v"] = True
                    touch(chunks)
                    cur["in_v"] = False

                    def fpe(e, tc=tc, bank=bank, aps=aps):
                        for k in range(KC):
                            ins = e.matmul(ps[:, bank, :], h[:, k, tc * 128:(tc + 1) * 128], aps[k],
                                           start=(k == 0), stop=(k == KC - 1))
                        return ins
                    tr.emit(PE, fpe, reads=list(wkeys) + hkeys(), writes=[("ps", bank)])
                    tr.emit(DVE, lambda e, tc=tc, hf=hf, bank=bank: e.bn_stats(out=vst[:, tc, hf, :], in_=ps[:, bank, :]),
                            reads=[("ps", bank)], writes=[("vst", tc, hf)])
                tr.emit(DVE, lambda e, tc=tc: e.bn_aggr(out=vmv[:, tc, :], in_=vst[:, tc, :, :].rearrange("p a b -> p (a b)")),
                        reads=[("vst", tc, 0), ("vst", tc, 1)], writes=[("vmv", tc)])
                tr.emit(ACT, lambda e, tc=tc: e.activation(out=vsc[:, tc, 0:1], in_=vmv[:, tc, 1:2], func=AF.Sqrt,
                                                           bias=epst[:, 0:1], scale=1.0),
                        reads=[("vmv", tc), "eps"], writes=[("vsc0", tc)])
                tr.emit(DVE, lambda e, tc=tc: e.reciprocal(out=vsc[:, tc, 0:1], in_=vsc[:, tc, 0:1]),
                        reads=[("vsc0", tc)], writes=[("vsc0", tc)])
                tr.emit(DVE, lambda e, tc=tc: e.scalar_tensor_tensor(
                    out=vsc[:, tc, 1:2], in0=vmv[:, tc, 0:1], scalar=-1.0, in1=vsc[:, tc, 0:1],
                    op0=ALU.mult, op1=ALU.mult), reads=[("vmv", tc), ("vsc0", tc)], writes=[("vsc1", tc)])
                for hf in range(2):
                    tr.emit(ACT, lambda e, tc=tc, hf=hf, bank=banks[hf]: e.activation(
                        out=vhat[:, tc, hf * 512:(hf + 1) * 512], in_=ps[:, bank, :], func=AF.Identity,
                        bias=vsc[:, tc, 1:2], scale=vsc[:, tc, 0:1]),
                        reads=[("ps", banks[hf]), ("vsc0", tc), ("vsc1", tc)], writes=[("vhat", tc, hf)])
            for m in range(KC):
                bank = next_bank()
                mm_group(ps[:, bank, :], OFF_U + m * 8, lambda k: h[:, k, 0:T], KC, hkeys(), ("ps", bank))
                tr.emit(ACT, lambda e, m=m, bank=bank: e.activation(out=u_sb[:, m, :], in_=ps[:, bank, :], func=AF.Copy),
                        reads=[("ps", bank)], writes=[("u", m)])
            def gv_pos(m):
                return OFF_GV + 16 * m + 8 * max(0, m - 3)

            def cu_pos(m):
                return gv_pos(m + 3) + 16 if m <= 4 else OFF_CU5 + (m - 5) * 8

            def conv_chunk(m):
                slot = m % NCST
                bank = next_bank()
                ap, key, G = wblk(cu_pos(m), 8)
                touch({G})

                def fpe(e):
                    for q in range(8):
                        for j in range(4):
                            ins = e.matmul(ps[32 * j:32 * j + 32, bank, :], ap[:, (q * 4 + j) * 32:(q * 4 + j + 1) * 32],
                                           cst[:, slot, j, 4 * q:4 * q + T], start=(q == 0), stop=(q == 7),
                                           tile_position=(0, 32 * j))
                    return ins
                tr.emit(PE, fpe, reads=[key, ("cst", slot)], writes=[("ps", bank)])
                tr.emit(ACT, lambda e: e.activation(out=y_sb[:, m, :], in_=ps[:, bank, :], func=AF.Identity,
                                                    bias=pcol(li, 24, m), scale=1.0),
                        reads=[("ps", bank), "pv"], writes=[("y", m)])
                tr.emit(ACT, lambda e: e.activation(out=sq[:, m, 0:T], in_=ps[:, bank, :], func=AF.Square,
                                                    bias=pcol(li, 24, m), scale=1.0),
                        reads=[("ps", bank), "pv"], writes=[("sq", m)])

            for m in range(KC):
                sl = m % 2
                cs_ = m % NCS
                bank_g = next_bank()
                hs_g = next_halo()
                mm_group(ps[:, bank_g, :], gv_pos(m), lambda k: h[:, k, 0:T], KC, hkeys(), ("ps", bank_g))
                mm_group(ps[:, hs_g, 0:2 * HALO], gv_pos(m), lambda k: h[:, k, T:TW], KC,
                         hkeys(), ("ps", hs_g))
                ts_ = next_tmp()
                tk = [("tmp", ts_, tc) for tc in range(4)]
                tr.emit(ACT, lambda e, ts_=ts_, bank=bank_g: e.activation(out=tmp[:, ts_, :], in_=ps[:, bank, :],
                                                                           func=AF.Sigmoid),
                        reads=[("ps", bank_g)], writes=tk)
                tr.emit(ACT, lambda e, sl=sl, hs=hs_g: e.activation(out=sgth[:, sl, 0:2 * HALO], in_=ps[:, hs, 0:2 * HALO],
                                                                     func=AF.Sigmoid),
                        reads=[("ps", hs_g)], writes=[("sgth", sl)])
                bank_v = next_bank()
                hs_v = next_halo()
                mm_group(ps[:, bank_v, :], gv_pos(m) + 8, lambda k: h[:, k, 0:T], KC, hkeys(), ("ps", bank_v))
                mm_group(ps[:, hs_v, 0:2 * HALO], gv_pos(m) + 8, lambda k: h[:, k, T:TW], KC,
                         hkeys(), ("ps", hs_v))
                tr.emit(DVE, lambda e, cs_=cs_, ts_=ts_, bank=bank_v: e.tensor_tensor(
                    out=c_sb[:, cs_, HALO:HALO + T], in0=ps[:, bank, :], in1=tmp[:, ts_, :], op=ALU.mult),
                    reads=[("ps", bank_v)] + tk, writes=[("c", cs_)])
                tr.emit(DVE, lambda e, cs_=cs_, sl=sl, hs=hs_v: e.tensor_tensor(
                    out=c_sb[:, cs_, 0:HALO], in0=ps[:, hs, 0:HALO], in1=sgth[:, sl, 0:HALO], op=ALU.mult),
                    reads=[("ps", hs_v), ("sgth", sl)], writes=[("ch0", cs_)])
                tr.emit(DVE, lambda e, cs_=cs_, sl=sl, hs=hs_v: e.tensor_tensor(
                    out=c_sb[:, cs_, HALO + T:TW], in0=ps[:, hs, HALO:2 * HALO],
                    in1=sgth[:, sl, HALO:2 * HALO], op=ALU.mult),
                    reads=[("ps", hs_v), ("sgth", sl)], writes=[("ch1", cs_)])
                slot = m % NCST
                fns = []
                ds = m % 4
                tr.emit_dma(SP, lambda e, cs_=cs_, ds=ds: e.dma_start(out=cdr[ds, :, :], in_=c_sb[:, cs_, :]),
                            "sem_cdw%d" % ds, reads=[("c", cs_), ("ch0", cs_), ("ch1", cs_), ("cpad",)],
                            writes=[("cdr", ds)])
                for s_ in range(4):
                    fns.append(lambda e, s_=s_, ds=ds, slot=slot: e.dma_start(
                        out=cst[32 * s_:32 * s_ + 32, slot, :, :],
                        in_=cdr[ds, :, s_:s_ + CSW].rearrange("(j c) x -> c j x", j=4)))
                tr.emit_dma_batch(SP, fns, "sem_cst%d" % slot, reads=[("cdr", ds)], writes=[("cst", slot)])
                if m >= 3:
                    conv_chunk(m - 3)
            for m in range(KC):
                for which, dstt, boff in ((0, siga, 8), (1, sigb, 16)):
                    bank = next_bank()
                    mm_group(ps[:, bank, :], OFF_GAB + m * 16 + which * 8, lambda k: h[:, k, 0:T], KC, hkeys(), ("ps", bank))
                    tr.emit(ACT, lambda e, m=m, bank=bank, dstt=dstt, boff=boff: e.activation(
                        out=dstt[:, m, :], in_=ps[:, bank, :], func=AF.Sigmoid, bias=pcol(li, boff, m), scale=1.0),
                        reads=[("ps", bank), "pv"], writes=[("sig", which, m)])
            for m in (5, 6, 7):
                conv_chunk(m)

        def stage_spatial(li):
            for g in range(KC):
                bank = next_bank()

                def fpe(e, g=g, bank=bank):
                    for tc in range(4):
                        ins = e.matmul(ps[:, bank, tc * 128:(tc + 1) * 128], vhat[:, tc, g * 128:(g + 1) * 128],
                                       wst_b[:, li * 1024 + g * 128: li * 1024 + (g + 1) * 128], start=True, stop=True)
                    return ins
                tr.emit(PE, fpe, reads=[("vhat", tc, g // 4) for tc in range(4)] + [("wst_b", li)], writes=[("ps", bank)])
                tslot = next_tmp()
                tr.emit(DVE, lambda e, g=g, bank=bank, tslot=tslot: e.scalar_tensor_tensor(
                    out=tmp[:, tslot, :].rearrange("p (a q) -> p a q", a=4),
                    in0=ps[:, bank, :].rearrange("p (a q) -> p a q", a=4),
                    scalar=pcol(li, 48, g),
                    in1=Bsp[:, li * 8 + g:li * 8 + g + 1, :].broadcast_to([128, 4, 128]),
                    op0=ALU.mult, op1=ALU.add),
                    reads=[("ps", bank), "pv", ("Bsp", li, g)], writes=[("tmp", tslot, tc) for tc in range(4)])
                tr.emit(DVE, lambda e, g=g, tslot=tslot: e.tensor_tensor(out=gm[:, g, :], in0=tmp[:, tslot, :],
                                                                           in1=u_sb[:, g, :], op=ALU.mult),
                        reads=[("tmp", tslot, tc) for tc in range(4)] + [("u", g)], writes=[("u", g)])

        def stage_conv_ln(li):
            b1 = next_bank()
            b2 = next_bank()

            def fpe1(e):
                for kc in range(KC):
                    ins = e.matmul(ps[:, b1, :], ones[:], y_sb[:, kc, :], start=(kc == 0), stop=(kc == KC - 1))
                return ins

            def fpe2(e):
                for kc in range(KC):
                    ins = e.matmul(ps[:, b2, :], ones[:], sq[:, kc, 0:T], start=(kc == 0), stop=(kc == KC - 1))
                return ins
            tr.emit(PE, fpe1, reads=["ones"] + [("y", m) for m in range(KC)], writes=[("ps", b1)])
            tr.emit(PE, fpe2, reads=["ones"] + [("sq", m) for m in range(KC)], writes=[("ps", b2)])
            tr.emit(DVE, lambda e: e.tensor_scalar(out=st_mean[:], in0=ps[:, b1, :], scalar1=1.0 / D, scalar2=None,
                                                   op0=ALU.mult), reads=[("ps", b1)], writes=["st_mean"])
            tr.emit(DVE, lambda e: e.tensor_tensor(out=st_mr[:], in0=st_mean[:], in1=st_mean[:], op=ALU.mult),
                    reads=["st_mean"], writes=["st_mr"])
            tr.emit(DVE, lambda e: e.scalar_tensor_tensor(out=st_rstd[:], in0=ps[:, b2, :], scalar=1.0 / D, in1=st_mr[:],
                                                          op0=ALU.mult, op1=ALU.subtract),
                    reads=[("ps", b2), "st_mr"], writes=["st_rstd"])
            tr.emit(ACT, lambda e: e.activation(out=st_rstd[:], in_=st_rstd[:], func=AF.Sqrt, bias=epst[:, 0:1], scale=1.0),
                    reads=["st_rstd", "eps"], writes=["st_rstd"])
            tr.emit(DVE, lambda e: e.reciprocal(out=st_rstd[:], in_=st_rstd[:]), reads=["st_rstd"], writes=["st_rstd"])
            tr.emit(DVE, lambda e: e.tensor_tensor(out=st_mr[:], in0=st_mean[:], in1=st_rstd[:], op=ALU.mult),
                    reads=["st_mean", "st_rstd"], writes=["st_mr"])
            for m in range(KC):
                tslot = next_tmp()
                tk = [("tmp", tslot, tc) for tc in range(4)]
                tr.emit(DVE, lambda e, m=m, tslot=tslot: e.tensor_tensor(out=tmp[:, tslot, :], in0=y_sb[:, m, :],
                                                                           in1=st_rstd[:], op=ALU.mult),
                        reads=[("y", m), "st_rstd"], writes=tk)
                tr.emit(DVE, lambda e, tslot=tslot: e.tensor_tensor(out=tmp[:, tslot, :], in0=tmp[:, tslot, :],
                                                                      in1=st_mr[:], op=ALU.subtract),
                        reads=tk + ["st_mr"], writes=tk)
                tr.emit(ACT, lambda e, m=m, tslot=tslot: e.activation(out=z_sb[:, m, :], in_=tmp[:, tslot, :], func=AF.Silu,
                                                                       bias=pcol(li, 40, m), scale=pcol(li, 32, m)),
                        reads=tk + ["pv"], writes=[("z", m)])

        def stage_mix_pairs(li):
            gk = [("u", k) for k in range(KC)]
            zk = [("z", k) for k in range(KC)]
            for m in range(KC):
                bank_b = next_bank()
                mm_group(ps[:, bank_b, :], OFF_MIX + m * 16, lambda k: gm[:, k, :], KC, gk, ("ps", bank_b))
                bank_a = next_bank()
                mm_group(ps[:, bank_a, :], OFF_MIX + m * 16 + 8, lambda k: z_sb[:, k, :], KC, zk, ("ps", bank_a))
                t1 = next_tmp()
                t2 = next_tmp()
                k1 = [("tmp", t1, tc) for tc in range(4)]
                k2 = [("tmp", t2, tc) for tc in range(4)]
                tr.emit(DVE, lambda e, m=m, bank=bank_b, t1=t1: e.tensor_tensor(out=tmp[:, t1, :], in0=ps[:, bank, :],
                                                                                 in1=sigb[:, m, :], op=ALU.mult),
                        reads=[("ps", bank_b), ("sig", 1, m)], writes=k1)
                tr.emit(DVE, lambda e, m=m, bank=bank_a, t2=t2: e.tensor_tensor(out=tmp[:, t2, :], in0=ps[:, bank, :],
                                                                                 in1=siga[:, m, :], op=ALU.mult),
                        reads=[("ps", bank_a), ("sig", 0, m)], writes=k2)
                tr.emit(DVE, lambda e, m=m, t1=t1, t2=t2: e.tensor_tensor(out=a_sb[:, m, :], in0=tmp[:, t1, :],
                                                                            in1=tmp[:, t2, :], op=ALU.add),
                        reads=k1 + k2, writes=[("a", m)])

        def stage_wo(b, li, after_each=None):
            mk = [("a", k) for k in range(KC)]
            for m in range(KC):
                bank = next_bank()
                mm_group(ps[:, bank, :], OFF_WO + m * 8, lambda k: a_sb[:, k, :], KC, mk, ("ps", bank))
                tr.emit(DVE, lambda e, m=m, bank=bank: e.tensor_tensor(out=xt[b][:, m, 0:T], in0=ps[:, bank, :],
                                                                        in1=xt[b][:, m, 0:T], op=ALU.add),
                        reads=[("ps", bank), ("xt", b, m)], writes=[("xt", b, m)])
                if after_each is not None:
                    after_each(m)

        def stage_ffn_up(li):
            for j in range(NJ):
                bank_g = next_bank()
                mm_group(ps[:, bank_g, :], OFF_FFN + j * 16, lambda k: h2[:, k, :], KC, hkeys("h2"), ("ps", bank_g))
                bank_u = next_bank()
                mm_group(ps[:, bank_u, :], OFF_FFN + j * 16 + 8, lambda k: h2[:, k, :], KC, hkeys("h2"), ("ps", bank_u))
                tslot = next_tmp()
                tk = [("tmp", tslot, tc) for tc in range(4)]
                tr.emit(ACT, lambda e, bank=bank_g, tslot=tslot: e.activation(out=tmp[:, tslot, :], in_=ps[:, bank, :],
                                                                               func=AF.Silu),
                        reads=[("ps", bank_g)], writes=tk)
                tr.emit(DVE, lambda e, j=j, bank=bank_u, tslot=tslot: e.tensor_tensor(
                    out=a_sb[:, j, :], in0=ps[:, bank, :], in1=tmp[:, tslot, :], op=ALU.mult),
                    reads=[("ps", bank_u)] + tk, writes=[("a", j)])

        def stage_ffn_down(b, li, after_each=None):
            ak = [("a", j) for j in range(NJ)]
            for m in range(KC):
                bank = next_bank()
                mm_group(ps[:, bank, :], OFF_DOWN + m * NJ, lambda k: a_sb[:, k, :], NJ, ak, ("ps", bank))
                tr.emit(DVE, lambda e, m=m, bank=bank: e.tensor_tensor(out=xt[b][:, m, 0:T], in0=ps[:, bank, :],
                                                                        in1=xt[b][:, m, 0:T], op=ALU.add),
                        reads=[("ps", bank), ("xt", b, m)], writes=[("xt", b, m)])
                if after_each is not None:
                    after_each(m)

        def stage_final_norm(b, after_each=None):
            rms_stats(b, False)
            for kc in range(KC):
                c = L * NPL + kc
                tr.emit(DVE, lambda e, kc=kc, c=c: e.scalar_tensor_tensor(
                    out=xt[b][:, kc, 0:T], in0=xt[b][:, kc, 0:T], scalar=pv[:, c:c + 1], in1=rstd[:, 0:T],
                    op0=ALU.mult, op1=ALU.mult), reads=[("xt", b, kc), "rstd_m", "pv"], writes=[("xt", b, kc)])
                if after_each is not None:
                    after_each(kc)

        NTL = len(seqn)

        def emit_casts_for(n):
            li, ti = seqn[n]
            if li + 1 < L:
                per = (NPIECE + ntiles - 1) // ntiles
                for c in range(ti * per, min((ti + 1) * per, NPIECE)):
                    emit_cast(li + 1, c)

        load_x(*seqn[0])
        cast0_upto(NPIECE)
        rms_stats(0, True)
        norm_to_h(0, seqn[0][0], 0, True, h, "h")
        for it in range(NTL + 1):
            cur["it"] = it
            hasA = it < NTL
            hasB = it >= 1
            if hasA:
                li, ti = seqn[it]
                b = it % 2
                emit_casts_for(it)
            if hasB:
                lb, tb = seqn[it - 1]
                bb = (it - 1) % 2
            if hasA:
                stage_proj(b, li)
                if it == 0:
                    cast0_upto(8)
            if hasB:
                stage_ffn_up(lb)
            if hasA:
                stage_spatial(li)
                stage_conv_ln(li)
            if hasB:
                st_hook = lambda kc, lb=lb, tb=tb: store_x_chunk(lb, tb, kc)
                if final_norm and lb == L - 1:
                    stage_ffn_down(bb, lb)
                    stage_final_norm(bb, st_hook)
                else:
                    stage_ffn_down(bb, lb, st_hook)
            if it + 1 < NTL:
                load_x(*seqn[it + 1])
                rms_squares((it + 1) % 2, True)
            if hasA:
                stage_mix_pairs(li)
                if it == 0:
                    cast0_upto(NPIECE)
            if it + 1 < NTL:
                rms_rstd(True)
            if hasA:
                nxt = (it + 1) % 2
                lnx = seqn[it + 1][0] if it + 1 < NTL else None
                hook = (lambda m: norm_to_h_one(nxt, lnx, 0, True, m, h, "h")) if it + 1 < NTL else None
                stage_wo(b, li, hook)
                rms_stats(b, False)
                norm_to_h(b, li, 64, False, h2, "h2")
        tr.emit_wait_only(SP, [k for ti in range(ntiles) for k in xd_keys(L, ti)])

        semkeys = set()
        for eng in engines:
            for waits, fn, sk, inc in eng.prog:
                for s, v in waits:
                    semkeys.add(s)
                if sk:
                    semkeys.add(sk)
        sems = {k: es.enter_context(nc.semaphore(k)) for k in sorted(semkeys)}

        def replay(eng, handle):
            for waits, fn, sk, inc in eng.prog:
                for s, v in waits:
                    handle.wait_ge(sems[s], v)
                if fn is not None:
                    ins = fn(handle)
                    ins.then_inc(sems[sk], inc)

        block = es.enter_context(nc.Block())
        block.tensor(lambda e: replay(PE, e))
        block.scalar(lambda e: replay(ACT, e))
        block.vector(lambda e: replay(DVE, e))
        block.gpsimd(lambda e: replay(POOL, e))
        block.sync(lambda e: replay(SP, e))
    return nc


def _prep_params(inputs, layers, with_final):
    L = len(layers)
    wall = np.stack([_layer_blocks(inputs["w_in"][l], inputs["conv_w"][l], inputs["w_conv_out"][l],
                                   inputs["w_sgu_out"][l], inputs["w_o"][l], inputs["w_ffn_gate"][l],
                                   inputs["w_ffn_up"][l], inputs["w_ffn_down"][l]) for l in layers])
    pvec = np.zeros((128, NPL * L + 8), np.float32)
    wst = np.zeros((128, L * 1024), np.float32)
    bsr = np.zeros((128, L * 1024), np.float32)
    for i, l in enumerate(layers):
        o = i * NPL
        pvec[:, o + 0:o + 8] = _vec8(inputs["norm_mix"][l])
        pvec[:, o + 8:o + 16] = _vec8(inputs["gate_bias"][l][:D])
        pvec[:, o + 16:o + 24] = _vec8(inputs["gate_bias"][l][D:])
        pvec[:, o + 24:o + 32] = _vec8(inputs["conv_b"][l])
        pvec[:, o + 32:o + 40] = _vec8(inputs["conv_ln_g"][l])
        pvec[:, o + 40:o + 48] = _vec8(inputs["conv_ln_b"][l])
        pvec[:, o + 48:o + 56] = _vec8(inputs["sgu_ln_g"][l])
        pvec[:, o + 56:o + 64] = _vec8(inputs["sgu_ln_b"][l])
        pvec[:, o + 64:o + 72] = _vec8(inputs["norm_ffn"][l])
        wst[:, i * 1024:(i + 1) * 1024] = inputs["w_spatial"][l].transpose(2, 0, 1).reshape(128, 1024)
        bsr[:, i * 1024:(i + 1) * 1024] = np.broadcast_to(inputs["b_spatial"][l].reshape(1, 1024), (128, 1024))
    if with_final:
        pvec[:, NPL * L:NPL * L + 8] = _vec8(inputs["norm_final"])
    return wall, pvec, wst, bsr


def _to_fm(x, ncores):
    B, S, _ = x.shape
    nseq = B // ncores
    xr = x.reshape(ncores, nseq * S, 8, 128)
    return np.ascontiguousarray(xr.transpose(0, 3, 2, 1))


def _from_fm(y, B, S):
    ncores = y.shape[0]
    return np.ascontiguousarray(y.transpose(0, 3, 2, 1)).reshape(B, S, D)


_PROG_CACHE = {}


def _get_prog(n_layers, nseq, seq, final_norm):
    key = (n_layers, nseq, seq, final_norm)
    if key not in _PROG_CACHE:
        _PROG_CACHE[key] = build_program(n_layers, nseq, seq, final_norm)
    return _PROG_CACHE[key]


def run_layers(xfm, inputs, layers, with_final, ncores, nseq, seq):
    wall, pvec, wst, bsr = _prep_params(inputs, layers, with_final)
    nc = _get_prog(len(layers), nseq, seq, with_final)
    in_maps = [{"xin": xfm[c], "wall": wall, "pvec": pvec, "wst": wst, "bsr": bsr} for c in range(ncores)]
    res = run_bass_kernel_spmd(nc, in_maps, core_ids=list(range(ncores)))
    return np.stack([np.asarray(r["yout"]) for r in res.results])


FUSED = True


def kernel(**inputs):
    inputs = {k: np.asarray(v, np.float32) for k, v in inputs.items()}
    x = inputs["x"]
    B, S, _ = x.shape
    depth = inputs["w_in"].shape[0]
    nseq = B // NCORES
    xfm = _to_fm(x, NCORES)
    if FUSED:
        y = run_layers(xfm, inputs, list(range(depth)), True, NCORES, nseq, S)
    else:
        y = xfm
        for l in range(depth):
            y = run_layers(y, inputs, [l], l == depth - 1, NCORES, nseq, S)
    return _from_fm(y, B, S).astype(np.float32)
```
